# Optimizing a Trainium2 kernel written in Bass

```python
import jax, jax.numpy as jnp
from jax import lax
import numpy as np

D_MODEL = 1024
BATCH = 8
SEQ = 4096
DEPTH = 1

GRID_W = 64
CTX_LEN = 256
DN_HEADS = 8
DN_HEAD_DIM = 128
DN_WIDTH = DN_HEADS * DN_HEAD_DIM
SHORT_CONV = 5
CHUNK = 64
POOL_GROUPS = 4
POOL_WINDOWS = (2, 4, 8, 16)
POOL_WIDTH = D_MODEL
POOL_GROUP_DIM = POOL_WIDTH // POOL_GROUPS
D_MIX = DN_WIDTH + POOL_WIDTH
PROJ_SIZES = (3 * DN_WIDTH, 4 * DN_HEADS, DN_WIDTH, POOL_WIDTH, POOL_WIDTH)
D_PROJ = sum(PROJ_SIZES)
PROJ_SPLITS = tuple(int(s) for s in np.cumsum(PROJ_SIZES)[:-1])
EPS = 1e-6

kernel_name = "hymba_gated_deltanet_multiscale_pool_prefix_ctx"


def _rms_norm(h, gain):
    hf = h.astype(jnp.float32)
    hf = hf * lax.rsqrt(jnp.mean(hf * hf, axis=-1, keepdims=True) + EPS)
    return hf.astype(h.dtype) * gain


def _l2_norm(h):
    hf = h.astype(jnp.float32)
    hf = hf * lax.rsqrt(jnp.sum(hf * hf, axis=-1, keepdims=True) + EPS)
    return hf.astype(h.dtype)


def _short_conv(u, w):
    k = w.shape[0]
    return lax.conv_general_dilated(
        u, w[:, None, :], window_strides=(1,), padding=[(k // 2, k // 2)],
        dimension_numbers=('NWC', 'WIO', 'NWC'), feature_group_count=u.shape[-1])


def _gated_delta_rule(q, k, v, g, beta, s0):
    out_dtype = v.dtype
    q, k, v, g, beta = (t.astype(jnp.float32) for t in (q, k, v, g, beta))
    bsz, nh, length, dk = q.shape
    dv = v.shape[-1]
    n = length // CHUNK
    q = q * dk ** -0.5
    q, k, v = (t.reshape(bsz, nh, n, CHUNK, -1) for t in (q, k, v))
    g = jnp.cumsum(g.reshape(bsz, nh, n, CHUNK), axis=-1)
    beta = beta.reshape(bsz, nh, n, CHUNK, 1)
    incl = jnp.tril(jnp.ones((CHUNK, CHUNK), bool))
    strict = jnp.tril(jnp.ones((CHUNK, CHUNK), bool), -1)
    decay = jnp.exp(jnp.where(incl, g[..., :, None] - g[..., None, :], -jnp.inf))
    kb = k * beta
    a_mat = jnp.where(strict, jnp.einsum('bhncd,bhnsd->bhncs', kb, k) * decay, 0.0)
    eye = jnp.eye(CHUNK, dtype=jnp.float32)
    rhs = jnp.concatenate([v * beta, kb * jnp.exp(g)[..., None]], axis=-1)
    sol = lax.linalg.triangular_solve(eye + a_mat, rhs, left_side=True, lower=True,
                                      unit_diagonal=True)
    u_c, w_c = sol[..., :dv], sol[..., dv:]
    qk = jnp.einsum('bhncd,bhnsd->bhncs', q, k) * decay
    q_dec = q * jnp.exp(g)[..., None]
    g_last = g[..., -1]
    k_tail = k * jnp.exp(g_last[..., None] - g)[..., None]
    xs = tuple(jnp.moveaxis(t, 2, 0) for t in (u_c, w_c, qk, q_dec, k_tail, g_last))

    def step(state, inp):
        u_i, w_i, qk_i, q_i, k_i, gl_i = inp
        v_new = u_i - jnp.einsum('bhcd,bhdv->bhcv', w_i, state)
        o_i = (jnp.einsum('bhcd,bhdv->bhcv', q_i, state)
               + jnp.einsum('bhcs,bhsv->bhcv', qk_i, v_new))
        state = state * jnp.exp(gl_i)[..., None, None] + jnp.einsum('bhcd,bhcv->bhdv', k_i, v_new)
        return state, o_i

    s_final, o = lax.scan(step, s0.astype(jnp.float32), xs)
    o = jnp.moveaxis(o, 0, 2).reshape(bsz, nh, length, dv)
    return o.astype(out_dtype), s_final


def _window_bounds(n, w):
    idx = jnp.arange(n)
    lo = jnp.clip(idx - w // 2, 0, n)
    hi = jnp.clip(idx - w // 2 + w, 0, n)
    return lo, hi


def _pool_grid(u, w):
    bsz, length, ch = u.shape
    rows = length // GRID_W
    grid = u.reshape(bsz, rows, GRID_W, ch)
    sat = jnp.pad(jnp.cumsum(jnp.cumsum(grid, axis=1), axis=2), ((0, 0), (1, 0), (1, 0), (0, 0)))
    lr, hr = _window_bounds(rows, w)
    lc, hc = _window_bounds(GRID_W, w)
    s_hr, s_lr = sat[:, hr], sat[:, lr]
    box = s_hr[:, :, hc] - s_lr[:, :, hc] - s_hr[:, :, lc] + s_lr[:, :, lc]
    cnt = ((hr - lr)[:, None] * (hc - lc)[None, :]).astype(u.dtype)
    return (box / cnt[None, :, :, None]).reshape(bsz, length, ch)


def _pool_seq(u, w):
    length = u.shape[1]
    cs = jnp.pad(jnp.cumsum(u, axis=1), ((0, 0), (1, 0), (0, 0)))
    lo, hi = _window_bounds(length, w)
    return (cs[:, hi] - cs[:, lo]) / (hi - lo).astype(u.dtype)[None, :, None]


def _pool_mixer(u, grid, pool_w, pool_scale):
    bsz, length, _ = u.shape
    pool = _pool_grid if grid else _pool_seq
    groups = jnp.split(u.astype(jnp.float32), POOL_GROUPS, axis=-1)
    d = jnp.stack([pool(gu, w) - gu for gu, w in zip(groups, POOL_WINDOWS)], axis=2)
    y = jnp.einsum('blgc,gce->blge', d.astype(u.dtype), pool_w).reshape(bsz, length, POOL_WIDTH)
    return y * pool_scale


def _mix(h, shift, scale, grid, s0_f, s0_b, gain, w_in, conv_w, a_log, dt_bias, dn_gain,
         pool_w, pool_scale):
    bsz, length, _ = h.shape
    hn = _rms_norm(h, gain) * (1 + scale) + shift
    proj = jnp.einsum('bld,dp->blp', hn, w_in)
    qkv, ab, z_dn, u_pool, z_pool = jnp.split(proj, PROJ_SPLITS, axis=-1)
    qkv = jax.nn.silu(_short_conv(qkv, conv_w))
    q, k, v = (t.reshape(bsz, length, DN_HEADS, DN_HEAD_DIM).transpose(0, 2, 1, 3)
               for t in jnp.split(qkv, 3, axis=-1))
    q, k = _l2_norm(q), _l2_norm(k)
    ab = ab.astype(jnp.float32).reshape(bsz, length, 2, 2, DN_HEADS).transpose(2, 3, 0, 4, 1)
    g = -jnp.exp(a_log.astype(jnp.float32))[:, None, :, None] * jax.nn.softplus(
        ab[0] + dt_bias.astype(jnp.float32)[:, None, :, None])
    beta = jax.nn.sigmoid(ab[1])
    o_f, s_f = _gated_delta_rule(q, k, v, g[0], beta[0], s0_f)
    o_b, s_b = _gated_delta_rule(jnp.flip(q, 2), jnp.flip(k, 2), jnp.flip(v, 2),
                                 jnp.flip(g[1], -1), jnp.flip(beta[1], -1), s0_b)
    o = _rms_norm(o_f + jnp.flip(o_b, 2), dn_gain)
    o = o.transpose(0, 2, 1, 3).reshape(bsz, length, DN_WIDTH) * jax.nn.silu(z_dn)
    y_pool = _pool_mixer(u_pool, grid, pool_w, pool_scale) * jax.nn.silu(z_pool)
    return jnp.concatenate([o, y_pool], axis=-1), s_f, s_b


def setup_inputs(seed: int = 0) -> dict:
    key = jax.random.key(seed)
    ks = jax.random.split(key, 20)
    f32 = jnp.float32
    nrm = lambda k, shape, s: jax.random.normal(k, shape, f32) * s
    a_init = jnp.log(jax.random.uniform(ks[10], (DEPTH, 2, DN_HEADS), f32, 1.0, 16.0))
    dt = jnp.exp(jax.random.uniform(ks[11], (DEPTH, 2, DN_HEADS), f32,
                                    float(np.log(1e-3)), float(np.log(1e-1))))
    dt_bias = dt + jnp.log(-jnp.expm1(-dt))
    return {
        "x": nrm(ks[0], (BATCH, SEQ, D_MODEL), 1.0),
        "c": nrm(ks[1], (BATCH, D_MODEL), 1.0),
        "ctx": nrm(ks[2], (BATCH, CTX_LEN, D_MODEL), 1.0),
        "c_ctx": nrm(ks[3], (D_MODEL,), 1.0),
        "w_mod": nrm(ks[4], (DEPTH, D_MODEL, 3 * D_MODEL), D_MODEL ** -0.5),
        "b_mod": nrm(ks[5], (DEPTH, 3 * D_MODEL), 0.01),
        "norm_gain": 1.0 + nrm(ks[6], (DEPTH, D_MODEL), 0.1),
        "w_in": nrm(ks[7], (DEPTH, D_MODEL, D_PROJ), D_MODEL ** -0.5),
        "conv_w": nrm(ks[8], (DEPTH, SHORT_CONV, 3 * DN_WIDTH), SHORT_CONV ** -0.5),
        "a_log": a_init,
        "dt_bias": dt_bias,
        "dn_norm_gain": 1.0 + nrm(ks[12], (DEPTH, DN_HEAD_DIM), 0.1),
        "pool_w": nrm(ks[13], (DEPTH, POOL_GROUPS, POOL_GROUP_DIM, POOL_GROUP_DIM),
                       POOL_GROUP_DIM ** -0.5),
        "pool_scale": 1.0 + nrm(ks[14], (DEPTH, POOL_WIDTH), 0.1),
        "w_out": nrm(ks[15], (DEPTH, D_MIX, D_MODEL), D_MIX ** -0.5),
        "final_gain": 1.0 + nrm(ks[16], (D_MODEL,), 0.1),
    }


def reference(x, c, ctx, c_ctx, w_mod, b_mod, norm_gain, w_in, conv_w, a_log, dt_bias,
              dn_norm_gain, pool_w, pool_scale, w_out, final_gain):
    bsz = x.shape[0]
    silu_c = jax.nn.silu(c)
    silu_cc = jax.nn.silu(c_ctx)
    s_zero = jnp.zeros((bsz, DN_HEADS, DN_HEAD_DIM, DN_HEAD_DIM), jnp.float32)
    for layer in range(DEPTH):
        mod_x = (silu_c @ w_mod[layer] + b_mod[layer])[:, None, :]
        shift_x, scale_x, gate_x = jnp.split(mod_x, 3, axis=-1)
        mod_c = silu_cc @ w_mod[layer] + b_mod[layer]
        shift_c, scale_c, gate_c = jnp.split(mod_c, 3, axis=-1)
        params = (norm_gain[layer], w_in[layer], conv_w[layer], a_log[layer], dt_bias[layer],
                  dn_norm_gain[layer], pool_w[layer], pool_scale[layer])
        y_ctx, s_f, s_b = _mix(ctx, shift_c, scale_c, False, s_zero, s_zero, *params)
        y_x, _, _ = _mix(x, shift_x, scale_x, True, s_f, s_b, *params)
        x = x + gate_x * jnp.einsum('blm,md->bld', y_x, w_out[layer])
        if layer < DEPTH - 1:
            ctx = ctx + gate_c * jnp.einsum('blm,md->bld', y_ctx, w_out[layer])
    return _rms_norm(x, final_gain)
```

```python
import os
from contextlib import ExitStack
import numpy as np
import ml_dtypes
import concourse.bass as bass
import concourse.mybir as mybir
from concourse.bass_utils import run_bass_kernel_spmd

F32 = mybir.dt.float32
BF16 = mybir.dt.bfloat16
AF = mybir.ActivationFunctionType
ALU = mybir.AluOpType
NPBF = ml_dtypes.bfloat16

D = 1024
L = 4096
LC = 256
NT = 34
EPS = 1e-6
NDS = 40


class TR:
    def __init__(self, nc, needed=None):
        self.nc = nc
        self.needed = needed
        self.wset = set()
        self.idx = {e: 0 for e in ('pe', 'act', 'dve', 'pool')}
        self.eng = {'pe': nc.tensor, 'act': nc.scalar, 'dve': nc.vector, 'pool': nc.gpsimd, 'sp': nc.sync}
        self.sem = {e: nc.alloc_semaphore('s_' + e) for e in ('pe', 'act', 'dve', 'pool')}
        self.cnt = {e: 0 for e in self.sem}
        self.dsem = [nc.alloc_semaphore('d%d' % i) for i in range(NDS)]
        self.dn = 0
        self.wsem = [nc.alloc_semaphore('w%d' % i) for i in range(56)]
        self.wn = 0
        self.waited = {e: {} for e in self.eng}
        self.lastw = {}
        self.reads = {}
        self.nwait = 0

    def _wait(self, e, ev):
        k, sem, val, src = ev
        if self.waited[e].get(k, 0) >= val:
            return
        if src == e and e == 'pe':
            return
        self.eng[e].wait_ge(sem, val)
        self.nwait += 1
        self.waited[e][k] = val
        if self.needed is None and src != 'dma':
            self.wset.add((k, val))

    def _deps(self, e, reads, writes):
        for r in reads:
            if r in self.lastw:
                self._wait(e, self.lastw[r])
        for w in writes:
            if w in self.lastw:
                self._wait(e, self.lastw[w])
            for ev in self.reads.get(w, ()):
                self._wait(e, ev)

    def _record(self, ev, reads, writes):
        for r in reads:
            lst = self.reads.setdefault(r, [])
            lst[:] = [x for x in lst if x[0] != ev[0]]
            lst.append(ev)
        for w in writes:
            self.lastw[w] = ev
            self.reads[w] = []

    def op(self, e, fn, reads=(), writes=()):
        self._deps(e, reads, writes)
        ins = fn()
        self.idx[e] += 1
        if self.needed is None or (e, self.idx[e]) in self.needed:
            self.cnt[e] += 1
            ins.then_inc(self.sem[e], 1)
            ev = (e, self.sem[e], self.cnt[e], e)
        else:
            ev = (e, self.sem[e], self.cnt[e] + 1, e)
        self._record(ev, reads, writes)
        return ev

    def dma(self, q, out, in_, reads=(), writes=()):
        i = self.dn % NDS
        val = 16 * (self.dn // NDS + 1)
        self.dn += 1
        key = 'd%d' % i
        if val > 16:
            self._wait(q, (key, self.dsem[i], val - 16, 'dma'))
        self._deps(q, reads, writes)
        self.eng[q].dma_start(out=out, in_=in_).then_inc(self.dsem[i], 16)
        ev = (key, self.dsem[i], val, 'dma')
        self._record(ev, reads, writes)
        return ev

    def dma_w(self, q, out, in_, reads=(), writes=()):
        i = self.wn
        self.wn += 1
        self._deps(q, reads, writes)
        self.eng[q].dma_start(out=out, in_=in_).then_inc(self.wsem[i], 16)
        ev = ('w%d' % i, self.wsem[i], 16, 'dma')
        self._record(ev, reads, writes)
        return ev

    def barrier(self, engines=('pe', 'act', 'dve', 'pool', 'sp')):
        for e in engines:
            for f in self.sem:
                if f != e and self.cnt[f] > 0:
                    self._wait(e, (f, self.sem[f], self.cnt[f], f))
            for i in range(min(self.dn, NDS)):
                n_used = (self.dn - 1 - i) // NDS + 1
                self._wait(e, ('d%d' % i, self.dsem[i], 16 * n_used, 'dma'))


def _consts():
    c = {}
    idx = np.arange(128)
    low_incl = (idx[None, :] <= idx[:, None]).astype(np.float32)
    up_incl = low_incl.T.copy()

    def bd(s):
        i = idx // s
        return (i[:, None] == i[None, :]).astype(np.float32)
    eye = np.eye(128, dtype=np.float32)
    lowm = [bd(16) * (low_incl - eye)] + [(bd(2 * s) - bd(s)) * low_incl for s in (16, 32, 64)]
    upm = [m.T.copy() for m in lowm]
    rep4 = lambda m: np.tile(m, (1, 4))
    c['ident_f'] = eye
    c['ones_f'] = np.ones((128, 128), np.float32)
    c['mcum'] = np.concatenate([up_incl, low_incl], 1)
    c['ident_b'] = eye.astype(NPBF)
    c['ones_b'] = np.ones((128, 128), NPBF)
    c['ident4_b'] = rep4(eye).astype(NPBF)
    lowm = [bd(16) * (low_incl - eye), (bd(32) - bd(16)) * low_incl, (bd(64) - bd(32)) * low_incl, (bd(128) - bd(64)) * low_incl]
    upm = [m.T.copy() for m in lowm]
    c['lvl'] = np.concatenate([rep4(m) for m in lowm] + [rep4(m) for m in upm], 1).astype(NPBF)
    c['bigm'] = np.concatenate([(1.0 - low_incl) * 30000.0, (1.0 - up_incl) * 30000.0], 1).astype(NPBF)
    return c


def _pool_mats():
    mats = []
    key2id = {}
    sched = [[[] for _ in range(4)] for _ in range(32)]
    for gi, w in enumerate((2, 4, 8, 16)):
        idx = np.arange(64)
        lo = np.clip(idx - w // 2, 0, 64)
        hi = np.clip(idx - w // 2 + w, 0, 64)
        P1 = np.zeros((64, 64), np.float64)
        for i in range(64):
            P1[i, lo[i]:hi[i]] = 1.0 / (hi[i] - lo[i])
        for t in range(32):
            rows_out = [2 * t, 2 * t + 1]
            for tin in range(32):
                rows_in = [2 * tin, 2 * tin + 1]
                blk = np.zeros((128, 128), np.float64)
                for a, rin in enumerate(rows_in):
                    for b_, rout in enumerate(rows_out):
                        blk[a * 64:(a + 1) * 64, b_ * 64:(b_ + 1) * 64] = P1[rout, rin] * P1.T
                if tin == t:
                    blk -= np.eye(128)
                if not np.any(blk):
                    continue
                m = blk.astype(np.float32).astype(NPBF)
                k = m.tobytes()
                if k not in key2id:
                    key2id[k] = len(mats)
                    mats.append(m)
                sched[t][gi].append((tin, key2id[k]))
    return np.stack(mats, 1), sched


_CACHE = {}


def build():
    if 'nc' in _CACHE:
        return _CACHE['nc']
    _CACHE.pop('tr', None)
    _build(None)
    needed = _CACHE['tr'].wset
    _CACHE.pop('nc', None)
    _build(needed)
    return _CACHE['nc']


def _build(needed):
    dbg = os.environ.get("MK_DEBUG", "")
    nc = bass.Bass("TRN2", target_bir_lowering=False)
    tr = TR(nc, needed)
    _CACHE['tr'] = tr
    C = _consts()
    pmats, psched = _pool_mats()
    NM = pmats.shape[1]

    def din(name, shape, dt=F32):
        return nc.dram_tensor(name, list(shape), dt, kind="ExternalInput").ap()

    def dscr(name, shape, dt):
        kind = "ExternalOutput" if name in dbg.split(",") else "Internal"
        return nc.dram_tensor(name, list(shape), dt, kind=kind).ap()

    x_d = din("x", [L, D])
    ctx_d = din("ctx", [LC, D])
    ccol_d = din("ccol", [128, 16])
    wmod_d = din("w_mod", [128, 8, 3072])
    bmod_d = din("b_mod", [1, 3072])
    gain_d = din("norm_gain", [1, D])
    win_d = din("w_in", [128, 8, 6176])
    convw_d = din("conv_w", [128, 24, 5])
    alog_d = din("a_log", [1, 16])
    dtb_d = din("dt_bias", [1, 16])
    dng_d = din("dn_gain", [128, 1])
    poolw_d = din("pool_w", [128, 4, 2, 256])
    pscale_d = din("pool_scale", [128, 8])
    wout_d = din("w_out", [128, 16, D])
    fg_d = din("final_gain", [1, D])
    cf_d = {k: din("c_" + k, v.shape, F32 if v.dtype == np.float32 else BF16) for k, v in C.items()}
    pm_d = din("pmats", [128, NM, 128], BF16)
    out_d = nc.dram_tensor("out", [L, D], F32, kind="ExternalOutput").ap()

    rawx_d = dscr("rawx", [24, 128, L + 4], F32)
    rawc_d = dscr("rawc", [24, 128, LC + 4], F32)
    qt_d = dscr("qt", [NT, 128, 8, 128], BF16)
    kt_d = dscr("kt", [NT, 128, 8, 128], BF16)
    ktok_d = dscr("ktok", [NT, 128, 8, 128], BF16)
    vtok_d = dscr("vtok", [NT, 128, 8, 128], BF16)
    zdn_d = dscr("zdn", [32, 128, 8, 128], BF16)
    zpl_d = dscr("zpl", [32, 128, 8, 128], BF16)
    utok_d = dscr("utok", [32, 128, D], BF16)
    o_d = dscr("osc", [2, 32, 128, D], F32)
    gb_d = dscr("gbdbg", [128, NT, 32], F32)

    scopes = [ExitStack()]

    def sb(name, shape, dt=F32):
        h = scopes[-1].enter_context(nc.sbuf_tensor("sb_" + name, list(shape), dt))
        return h.ap() if hasattr(h, "ap") and callable(h.ap) else h

    def new_scope():
        scopes.append(ExitStack())

    def end_scope():
        scopes.pop().close()

    psum = nc.alloc_psum_tensor("ps", [128, 8 * 512], F32).ap()

    def bank(b, n=512):
        return psum[:, b * 512:b * 512 + n]

    def bankbf(b):
        return psum[:, b * 512:(b + 1) * 512].bitcast(BF16)

    def PB(b):
        return ('ps', b)

    eng = tr.eng

    cs = {}
    for k, v in C.items():
        cs[k] = sb("k_" + k, v.shape, F32 if v.dtype == np.float32 else BF16)
        tr.dma('sp', cs[k], cf_d[k], writes=[('c', k)])
    gb = sb("gb", [128, NT, 32])
    small = sb("small", [128, 64])
    epsc = small[:, 0:1]
    tr.op('dve', lambda: nc.vector.memset(epsc, EPS), writes=['epsc'])
    convw = sb("convw", [128, 24, 5])
    tr.dma('sp', convw, convw_d, writes=['convw'])
    dng = sb("dng", [128, 1])
    tr.dma('sp', dng, dng_d, writes=['dng'])
    pscale = sb("pscale", [128, 8])
    tr.dma('sp', pscale, pscale_d, writes=['pscale'])
    adt = sb("adt", [128, 32])
    tr.dma('sp', adt[:, 0:16], alog_d.partition_broadcast(128), writes=['adt'])
    tr.dma('sp', adt[:, 16:32], dtb_d.partition_broadcast(128), writes=['adt'])
    nexpa = sb("nexpa", [128, 16])
    tr.op('act', lambda: nc.scalar.activation(out=nexpa, in_=adt[:, 0:16], func=AF.Exp), reads=['adt'], writes=['nexpa'])
    tr.op('dve', lambda: nc.vector.tensor_scalar(out=nexpa, in0=nexpa, scalar1=-1.0, scalar2=None, op0=ALU.mult),
          reads=['nexpa'], writes=['nexpa'])
    gate_d = dscr("gate_scr", [1, D], F32)
    new_scope()
    bcast = sb("bcast", [128, 5, D])
    winb = sb("winb", [128, 8, 6176], BF16)
    for hh in range(4):
        for k in range(8):
            tr.dma_w('pool', winb[:, k, hh * 1544:(hh + 1) * 1544], win_d[:, k, hh * 1544:(hh + 1) * 1544], writes=[('winb', hh, k)])

    new_scope()
    wmr = [sb("wmr%d" % i, [128, 8, 512]) for i in range(2)]
    ccol = sb("ccol", [128, 16])
    tr.dma('sp', ccol, ccol_d, writes=['ccol'])
    rows = sb("rows", [1, 2, 3072])
    brow = sb("brow", [1, 3072])
    grow = sb("grow", [1, D])
    tr.dma('sp', brow, bmod_d, writes=['brow'])
    tr.dma('sp', grow, gain_d, writes=['grow'])
    scol = sb("scol", [128, 16])
    tr.op('act', lambda: nc.scalar.activation(out=scol, in_=ccol, func=AF.Silu), reads=['ccol'], writes=['scol'])
    for cg in range(6):
        with nc.allow_non_contiguous_dma(reason="w_mod column group"):
            tr.dma('sp', wmr[cg % 2], wmod_d[:, :, cg * 512:(cg + 1) * 512], writes=[('wmr', cg % 2)])
        for which in range(2):
            b = (which * 6 + cg) % 8
            for k in range(8):
                tr.op('pe', lambda k=k, b=b, which=which: nc.tensor.matmul(bank(b)[0:1, :], lhsT=scol[:, which * 8 + k:which * 8 + k + 1],
                                                                           rhs=wmr[cg % 2][:, k, :], start=(k == 0), stop=(k == 7)),
                      reads=['scol', ('wmr', cg % 2)], writes=[PB(b)])
            tr.op('dve', lambda b=b: nc.vector.tensor_tensor(out=rows[0:1, which, cg * 512:(cg + 1) * 512], in0=bank(b)[0:1, :],
                                                             in1=brow[0:1, cg * 512:(cg + 1) * 512], op=ALU.add),
                  reads=[PB(b), 'brow'], writes=['rows'])
    for which in range(2):
        tr.op('dve', lambda which=which: nc.vector.scalar_tensor_tensor(out=rows[0:1, which, 1024:2048], in0=rows[0:1, which, 1024:2048],
                                                                        scalar=1.0, in1=grow[0:1, :], op0=ALU.add, op1=ALU.mult),
              reads=['rows', 'grow'], writes=['rows'])
    tr.dma('sp', gate_d, rows[0:1, 0, 2048:3072], reads=['rows'], writes=['gate_d'])
    bl = [(0, 0, 1024), (1, 0, 0), (3, 1, 1024), (4, 1, 0)]
    ones_row = cs['ones_f'][0:1, :]
    nb = 0
    for slot, which, off in bl:
        for half in range(2):
            b = nb % 8
            nb += 1
            tr.op('pe', lambda b=b, which=which, off=off, half=half: nc.tensor.matmul(
                bank(b), lhsT=ones_row, rhs=rows[0:1, which, off + half * 512: off + (half + 1) * 512], start=True, stop=True),
                reads=['rows', ('c', 'ones_f')], writes=[PB(b)])
            tr.op('act', lambda b=b, slot=slot, half=half: nc.scalar.copy(out=bcast[:, slot, half * 512:(half + 1) * 512], in_=bank(b)),
                  reads=[PB(b)], writes=[('bcast', slot)])
    tr.barrier()
    end_scope()
    new_scope()

    ztile = sb("ztile", [128, 24, 2])
    tr.op('pool', lambda: nc.gpsimd.memset(ztile, 0.0), writes=['ztile'])
    with nc.allow_non_contiguous_dma(reason="halo zero fill"):
        for rd, ll in ((rawx_d, L), (rawc_d, LC)):
            tr.dma('sp', rd[:, :, 0:2].rearrange("b p t -> p b t"), ztile, reads=['ztile'], writes=['rawhalo'])
            tr.dma('sp', rd[:, :, ll + 2:ll + 4].rearrange("b p t -> p b t"), ztile, reads=['ztile'], writes=['rawhalo'])

    xb = [sb("xb%d" % i, [128, D]) for i in range(2)]
    junk = sb("junk", [128, D], BF16)
    h1 = [sb("h1_%d" % i, [128, D]) for i in range(2)]
    h2 = [sb("h2_%d" % i, [128, D], BF16) for i in range(2)]
    hnT = [sb("hnT%d" % i, [128, 8, 512], BF16) for i in range(2)]
    st32 = [sb("st32_%d" % i, [128, 512]) for i in range(6)]
    st16 = [sb("st16_%d" % i, [128, 512], BF16) for i in range(5)]
    stat = sb("stat", [128, 2, 8])
    abt = sb("abt", [128, 2, 64])

    h2x = [sb("h2x_%d" % i, [128, D], BF16) for i in range(6)]
    h2all = h2 + h2x
    seqs = [(ctx_d, rawc_d, 2, 0, False), (x_d, rawx_d, 32, 2, True)]
    mts = []
    for src_d, raw_d, ntiles, tbase, isx in seqs:
        for mt in range((ntiles + 3) // 4):
            mts.append((src_d, raw_d, min(4, ntiles - 4 * mt), tbase, isx, mt))
    st = {'nxt': 0, 'nbk': 0, 'n32': 0, 'n16': 0}
    def wq(c0, c1):
        return [('winb', hh, k) for hh in range(c0 // 1544, (c1 - 1) // 1544 + 1) for k in range(8)]

    def a_front(kk):
        src_d, raw_d, nt, tbase, isx, mt = mts[kk]
        gmslot, shslot = (0, 1) if isx else (3, 4)
        for i in range(nt):
            tl = 4 * mt + i
            p = st['nxt'] % 2
            st['nxt'] += 1
            hi = (kk % 2) * 4 + i
            xt = xb[p]
            tr.dma('sp', xt, src_d[tl * 128:(tl + 1) * 128, :], writes=[('xb', p)])
            tr.op('act', lambda xt=xt, p=p: nc.scalar.activation(out=junk, in_=xt, func=AF.Square, accum_out=stat[:, p, 0:1]),
                  reads=[('xb', p)], writes=['junk', ('stat', p)])
            tr.op('dve', lambda p=p: nc.vector.tensor_scalar(out=stat[:, p, 1:2], in0=stat[:, p, 0:1], scalar1=1.0 / D, scalar2=EPS,
                                                              op0=ALU.mult, op1=ALU.add), reads=[('stat', p)], writes=[('stat', p)])
            tr.op('act', lambda p=p: nc.scalar.activation(out=stat[:, p, 2:3], in_=stat[:, p, 1:2], func=AF.Sqrt),
                  reads=[('stat', p)], writes=[('stat', p)])
            tr.op('dve', lambda p=p: nc.vector.reciprocal(out=stat[:, p, 3:4], in_=stat[:, p, 2:3]), reads=[('stat', p)], writes=[('stat', p)])
            tr.op('dve', lambda p=p, xt=xt: nc.vector.scalar_tensor_tensor(out=h1[p], in0=xt, scalar=stat[:, p, 3:4], in1=bcast[:, gmslot, :],
                                                                           op0=ALU.mult, op1=ALU.mult),
                  reads=[('xb', p), ('stat', p), ('bcast', gmslot)], writes=[('h1', p)])
            tr.op('pool', lambda p=p, hi=hi: nc.gpsimd.tensor_tensor(out=h2all[hi], in0=h1[p], in1=bcast[:, shslot, :], op=ALU.add),
                  reads=[('h1', p), ('bcast', shslot)], writes=[('h2', hi)])

    def a_trans(kk):
        src_d, raw_d, nt, tbase, isx, mt = mts[kk]
        hT = hnT[kk % 2]
        hkey = ('hnT', kk % 2)
        for i in range(nt):
            hi = (kk % 2) * 4 + i
            b = st['nbk'] % 8
            st['nbk'] += 1
            for k in range(8):
                tr.op('pe', lambda k=k, b=b, hi=hi: nc.tensor.transpose(bankbf(b)[:, k * 128:(k + 1) * 128], h2all[hi][:, k * 128:(k + 1) * 128], cs['ident_b']),
                      reads=[('h2', hi), ('c', 'ident_b')], writes=[PB(b)])
            tr.op('act', lambda b=b, i=i, hT=hT: nc.scalar.copy(out=hT[:, :, i * 128:(i + 1) * 128],
                                                                in_=bankbf(b).rearrange("p (k t) -> p k t", k=8)),
                  reads=[PB(b)], writes=[hkey])

    def a_proj(kk, part):
        src_d, raw_d, nt, tbase, isx, mt = mts[kk]
        N = nt * 128
        hT = hnT[kk % 2]
        hkey = ('hnT', kk % 2)
        blocks = [(j, j * 128, 'raw') for j in range(24)]
        if isx:
            blocks += [(j, 3104 + j * 128, 'zdn') for j in range(8)] + [(j, 5152 + j * 128, 'zpl') for j in range(8)]
        half = len(blocks) // 2
        sel = blocks[:half] if part == 0 else blocks[half:]
        for bi, (j, col0, kind) in enumerate(sel):
            b = st['nbk'] % 8
            st['nbk'] += 1
            for k in range(8):
                tr.op('pe', lambda k=k, b=b, col0=col0: nc.tensor.matmul(bank(b, N), lhsT=winb[:, k, col0:col0 + 128], rhs=hT[:, k, 0:N],
                                                                        start=(k == 0), stop=(k == 7)),
                      reads=[hkey] + wq(col0, col0 + 128), writes=[PB(b)])
            e = 'act' if (bi % 2 == 0 or kind != 'raw') else 'dve'
            if kind == 'raw':
                s_ = st['n32'] % 6
                st['n32'] += 1
                if e == 'act':
                    tr.op('act', lambda b=b, s_=s_: nc.scalar.copy(out=st32[s_][:, 0:N], in_=bank(b, N)), reads=[PB(b)], writes=[('st32', s_)])
                else:
                    tr.op('dve', lambda b=b, s_=s_: nc.vector.tensor_copy(out=st32[s_][:, 0:N], in_=bank(b, N)), reads=[PB(b)], writes=[('st32', s_)])
                tr.dma('sp', raw_d[j, :, 2 + mt * 512: 2 + mt * 512 + N], st32[s_][:, 0:N], reads=[('st32', s_)], writes=['raw'])
            else:
                s_ = st['n16'] % 5
                st['n16'] += 1
                tr.op('act', lambda b=b, s_=s_: nc.scalar.activation(out=st16[s_][:, 0:N], in_=bank(b, N), func=AF.Silu),
                      reads=[PB(b)], writes=[('st16', s_)])
                zd = zdn_d if kind == 'zdn' else zpl_d
                tr.dma('sp', zd[4 * mt:4 * mt + nt, :, j, :].rearrange("t p k -> p t k"),
                       st16[s_][:, 0:N].rearrange("p (t k) -> p t k", t=nt), reads=[('st16', s_)], writes=[kind])
        if part == 0:
            return
        for i in range(nt):
            tl = 4 * mt + i
            if isx:
                for cg in range(2):
                    b = st['nbk'] % 8
                    st['nbk'] += 1
                    for k in range(8):
                        tr.op('pe', lambda k=k, b=b, cg=cg, i=i: nc.tensor.matmul(bank(b), lhsT=hT[:, k, i * 128:(i + 1) * 128],
                                                                                  rhs=winb[:, k, 4128 + cg * 512: 4128 + (cg + 1) * 512],
                                                                                  start=(k == 0), stop=(k == 7)),
                              reads=[hkey] + wq(4128, 5152), writes=[PB(b)])
                    s_ = st['n16'] % 5
                    st['n16'] += 1
                    tr.op('dve', lambda b=b, s_=s_: nc.vector.tensor_copy(out=st16[s_], in_=bank(b)), reads=[PB(b)], writes=[('st16', s_)])
                    tr.dma('sp', utok_d[tl, :, cg * 512:(cg + 1) * 512], st16[s_], reads=[('st16', s_)], writes=['utok'])
            b = st['nbk'] % 8
            st['nbk'] += 1
            for k in range(8):
                tr.op('pe', lambda k=k, b=b, i=i: nc.tensor.matmul(bank(b, 32), lhsT=hT[:, k, i * 128:(i + 1) * 128], rhs=winb[:, k, 3072:3104],
                                                                   start=(k == 0), stop=(k == 7)),
                      reads=[hkey] + wq(3072, 3104), writes=[PB(b)])
            q = tl % 2
            gt = tbase + tl
            ak = ('abt', q)
            tr.op('dve', lambda b=b, q=q: nc.vector.tensor_tensor(out=abt[:, q, 0:16], in0=bank(b, 32)[:, 0:16], in1=adt[:, 16:32], op=ALU.add),
                  reads=[PB(b), 'adt'], writes=[ak])
            tr.op('act', lambda q=q: nc.scalar.activation(out=abt[:, q, 16:32], in_=abt[:, q, 0:16], func=AF.Exp), reads=[ak], writes=[ak])
            tr.op('act', lambda q=q: nc.scalar.activation(out=abt[:, q, 32:48], in_=abt[:, q, 16:32], func=AF.Ln, bias=1.0, scale=1.0),
                  reads=[ak], writes=[ak])
            tr.op('dve', lambda q=q, gt=gt: nc.vector.tensor_tensor(out=gb[:, gt, 0:16], in0=abt[:, q, 32:48], in1=nexpa, op=ALU.mult),
                  reads=[ak, 'nexpa'], writes=['gb'])
            tr.op('act', lambda b=b, q=q: nc.scalar.activation(out=abt[:, q, 48:64], in_=bank(b, 32)[:, 16:32], func=AF.Exp, scale=-1.0),
                  reads=[PB(b)], writes=[ak])
            tr.op('dve', lambda q=q: nc.vector.tensor_scalar(out=abt[:, q, 48:64], in0=abt[:, q, 48:64], scalar1=1.0, scalar2=None, op0=ALU.add),
                  reads=[ak], writes=[ak])
            tr.op('dve', lambda q=q, gt=gt: nc.vector.reciprocal(out=gb[:, gt, 16:32], in_=abt[:, q, 48:64]), reads=[ak], writes=['gb'])

    a_front(0)
    a_trans(0)
    for kk in range(len(mts)):
        if kk + 1 < len(mts):
            a_front(kk + 1)
        a_proj(kk, 0)
        if kk + 1 < len(mts):
            a_trans(kk + 1)
        a_proj(kk, 1)
    nbk = st['nbk']
    if 'gbdbg' in dbg:
        tr.dma('sp', gb_d, gb, reads=['gb'], writes=['gbd'])
    tr.barrier()
    _CACHE['after_A'] = True

    if os.environ.get("MK_STOP", "") == "A":
        tr.op('dve', lambda: nc.vector.memset(st32[0], 0.0), writes=[('st32', 0)])
        for t in range(32):
            for hf in range(2):
                tr.dma('sp', out_d[t * 128:(t + 1) * 128, hf * 512:(hf + 1) * 512], st32[0], reads=[('st32', 0)], writes=['out'])
        tr.barrier()
        _CACHE['nc'] = nc
        print("instr waits", tr.nwait, "cnt", tr.cnt, "dma", tr.dn)
        return nc
    end_scope()
    end_scope()
    new_scope()
    rawb = [sb("rawb%d" % i, [128, 516]) for i in range(4)]
    accb = [sb("accb%d" % i, [128, 512]) for i in range(4)]
    sil = [sb("sil%d" % i, [128, 512]) for i in range(8)]
    sqb = [sb("sqb%d" % i, [128, 512], BF16) for i in range(2)]
    sdb = [sb("sdb%d" % i, [128, 512]) for i in range(8)]
    nrm = [sb("nrm%d" % i, [128, 512], BF16) for i in range(6)]
    vnr = [sb("vnr%d" % i, [128, 512], BF16) for i in range(8)]
    tkb = [sb("tkb%d" % i, [128, 512], BF16) for i in range(6)]
    dg = sb("dg", [128, 72, 128])
    for blk in range(24):
        for j in range(3):
            if (blk * 3 + j) % 2 == 0:
                tr.op('act', lambda blk=blk, j=j: nc.scalar.activation(out=dg[:, blk * 3 + j, :], in_=cs['ident_f'], func=AF.Copy, scale=convw[:, blk, j:j + 1]),
                      reads=['convw', ('c', 'ident_f')], writes=['dg'])
            else:
                tr.op('dve', lambda blk=blk, j=j: nc.vector.tensor_scalar(out=dg[:, blk * 3 + j, :], in0=cs['ident_f'], scalar1=convw[:, blk, j:j + 1],
                                                                         scalar2=None, op0=ALU.mult),
                      reads=['convw', ('c', 'ident_f')], writes=['dg'])
    items = []
    for raw_d, ntiles, tbase in ((rawc_d, 2, 0), (rawx_d, 32, 2)):
        for mt in range((ntiles + 3) // 4):
            nt = min(4, ntiles - 4 * mt)
            for blk in range(24):
                items.append((raw_d, mt, nt, tbase + 4 * mt, blk))
    cnt = {'nn': 0, 'ntk': 0, 'nbk': nbk}

    def b1_front(ix):
        raw_d, mt, nt, gt0, blk = items[ix]
        N = nt * 128
        r_ = ix % 4
        a_ = ix % 4
        tr.dma('sp', rawb[r_][:, 0:N + 4], raw_d[blk, :, mt * 512: mt * 512 + N + 4], reads=['raw', 'rawhalo'], writes=[('rawb', r_)])
        b = ix % 2
        for j in range(3):
            tr.op('pe', lambda j=j, b=b: nc.tensor.matmul(bank(b, N), lhsT=dg[:, blk * 3 + j, :], rhs=rawb[r_][:, j:j + N], start=(j == 0), stop=(j == 2)),
                  reads=[('rawb', r_), 'dg'], writes=[PB(b)])
        acc = accb[a_]
        tr.op('dve', lambda: nc.vector.scalar_tensor_tensor(out=acc[:, 0:N], in0=rawb[r_][:, 3:3 + N], scalar=convw[:, blk, 3:4], in1=bank(b, N),
                                                           op0=ALU.mult, op1=ALU.add),
              reads=[('rawb', r_), 'convw', PB(b)], writes=[('accb', a_)])
        tr.op('dve', lambda: nc.vector.scalar_tensor_tensor(out=acc[:, 0:N], in0=rawb[r_][:, 4:4 + N], scalar=convw[:, blk, 4:5], in1=acc[:, 0:N],
                                                           op0=ALU.mult, op1=ALU.add),
              reads=[('rawb', r_), 'convw', ('accb', a_)], writes=[('accb', a_)])

    NB8 = 8
    memo = {}

    def b1_s1(ixs):
        for ix in ixs:
            raw_d, mt, nt, gt0, blk = items[ix]
            N = nt * 128
            a_ = ix % 4
            s_ = ix % NB8
            acc = accb[a_]
            if blk < 16:
                tr.op('act', lambda acc=acc, s_=s_, N=N: nc.scalar.activation(out=sil[s_][:, 0:N], in_=acc[:, 0:N], func=AF.Silu),
                      reads=[('accb', a_)], writes=[('sil', s_)])
                q_ = ix % 2
                tr.op('pool', lambda s_=s_, N=N, q_=q_: nc.gpsimd.tensor_tensor(out=sqb[q_][:, 0:N], in0=sil[s_][:, 0:N], in1=sil[s_][:, 0:N], op=ALU.mult),
                      reads=[('sil', s_)], writes=[('sqb', q_)])
                b2_ = 2 + (ix % 4)
                tr.op('pe', lambda b2_=b2_, N=N, q_=q_: nc.tensor.matmul(bank(b2_, N), lhsT=cs['ones_b'], rhs=sqb[q_][:, 0:N], start=True, stop=True),
                      reads=[('sqb', q_), ('c', 'ones_b')], writes=[PB(b2_)])
                memo[('ss', ix)] = b2_
            else:
                tr.op('act', lambda acc=acc, s_=s_, N=N: nc.scalar.activation(out=vnr[s_][:, 0:N], in_=acc[:, 0:N], func=AF.Silu),
                      reads=[('accb', a_)], writes=[('vnr', s_)])

    def b1_s2(ixs):
        todo = [(ix, memo.pop(('ss', ix))) for ix in ixs if ('ss', ix) in memo]
        for ix, b2_ in todo:
            N = items[ix][2] * 128
            s_ = ix % NB8
            tr.op('act', lambda b2_=b2_, s_=s_, N=N: nc.scalar.activation(out=sdb[s_][:, 0:N], in_=bank(b2_, N), func=AF.Ln, bias=epsc, scale=1.0),
                  reads=[PB(b2_), 'epsc'], writes=[('sdb', s_)])
        for ix, b2_ in todo:
            N = items[ix][2] * 128
            s_ = ix % NB8
            tr.op('act', lambda s_=s_, N=N: nc.scalar.activation(out=sdb[s_][:, 0:N], in_=sdb[s_][:, 0:N], func=AF.Exp, scale=-0.5),
                  reads=[('sdb', s_)], writes=[('sdb', s_)])

    def b1_s3(ix):
        raw_d, mt, nt, gt0, blk = items[ix][:5]
        N = nt * 128
        s_ = ix % NB8
        head = blk % 8
        if blk < 16:
            n_ = cnt['nn'] % 6
            cnt['nn'] += 1
            cst = (128.0 ** -0.5) if blk < 8 else 1.0
            tr.op('dve', lambda: nc.vector.scalar_tensor_tensor(out=nrm[n_][:, 0:N], in0=sil[s_][:, 0:N], scalar=cst, in1=sdb[s_][:, 0:N],
                                                               op0=ALU.mult, op1=ALU.mult),
                  reads=[('sil', s_), ('sdb', s_)], writes=[('nrm', n_)])
            dst = qt_d if blk < 8 else kt_d
            memo.setdefault(('st', ix), []).append((dst[gt0:gt0 + nt, :, head, :].rearrange("t p k -> p t k"),
                                                    nrm[n_][:, 0:N].rearrange("p (t k) -> p t k", t=nt), ('nrm', n_)))
            srcb, skey = nrm[n_], ('nrm', n_)
        else:
            srcb, skey = vnr[s_], ('vnr', s_)
        if blk >= 8:
            b = 6 + (ix % 2)
            for i in range(nt):
                tr.op('pe', lambda i=i: nc.tensor.transpose(bankbf(b)[:, i * 128:(i + 1) * 128], srcb[:, i * 128:(i + 1) * 128], cs['ident_b']),
                      reads=[skey, ('c', 'ident_b')], writes=[PB(b)])
            memo[('tp', ix)] = b

    def b1_s4(ix):
        raw_d, mt, nt, gt0, blk = items[ix][:5]
        if blk < 8:
            return
        N = nt * 128
        head = blk % 8
        b = memo.pop(('tp', ix))
        k_ = cnt['ntk'] % 6
        cnt['ntk'] += 1
        if blk < 16:
            tr.op('dve', lambda: nc.vector.tensor_copy(out=tkb[k_][:, 0:N], in_=bankbf(b)[:, 0:N]), reads=[PB(b)], writes=[('tkb', k_)])
        else:
            tr.op('act', lambda: nc.scalar.copy(out=tkb[k_][:, 0:N], in_=bankbf(b)[:, 0:N]), reads=[PB(b)], writes=[('tkb', k_)])
        dst = ktok_d if blk < 16 else vtok_d
        memo.setdefault(('st', ix), []).append((dst[gt0:gt0 + nt, :, head, :].rearrange("t p k -> p t k"),
                                                tkb[k_][:, 0:N].rearrange("p (t k) -> p t k", t=nt), ('tkb', k_)))

    def b1_s5(ix):
        for dst_ap, src_ap, key in memo.pop(('st', ix), []):
            tr.dma('sp', dst_ap, src_ap, reads=[key], writes=['qkt'])

    nit = len(items)
    for n in range(nit + 12):
        if n < nit:
            b1_front(n)
        m1 = n - 2
        if m1 % 2 == 1 and 0 <= m1 < nit:
            b1_s1([m1 - 1, m1])
        m2 = n - 4
        if m2 % 2 == 1 and 0 <= m2 < nit:
            b1_s2([m2 - 1, m2])
        if 0 <= n - 7 < nit:
            b1_s3(n - 7)
        if 0 <= n - 8 < nit:
            b1_s4(n - 8)
        if 0 <= n - 10 < nit:
            b1_s5(n - 10)
    nbk = cnt['nbk']
    tr.barrier()

    if os.environ.get("MK_STOP", "") == "B1":
        tr.op('dve', lambda: nc.vector.memset(accb[0], 0.0), writes=[('accb', 0)])
        for t in range(32):
            for hf in range(2):
                tr.dma('sp', out_d[t * 128:(t + 1) * 128, hf * 512:(hf + 1) * 512], accb[0], reads=[('accb', 0)], writes=['out'])
        tr.barrier()
        _CACHE['nc'] = nc
        print("instr waits", tr.nwait, "cnt", tr.cnt, "dma", tr.dn)
        return nc
    end_scope()
    new_scope()
    Sf = sb("Sf", [128, 16, 128])
    Sb = sb("Sb", [128, 16, 128], BF16)
    tr.op('dve', lambda: nc.vector.memset(Sf, 0.0), writes=[('Sf', c_) for c_ in range(4)])
    tr.op('pool', lambda: nc.gpsimd.memset(Sb, 0.0), writes=[('Sb', c_) for c_ in range(4)])
    ldb = {}
    for d_ in range(2):
        for p_ in range(2):
            for nm in ('qt', 'kt', 'ktok', 'vtok'):
                ldb[(d_, p_, nm)] = sb("ld_%s_%d_%d" % (nm, d_, p_), [128, 8, 128], BF16)
    scal = {(d_, p_): sb("scal_%d_%d" % (d_, p_), [128, 64]) for d_ in range(2) for p_ in range(2)}
    ost1 = {d_: sb("ost_%d" % d_, [128, D]) for d_ in range(2)}
    ost = {(d_, p_): ost1[d_] for d_ in range(2) for p_ in range(2)}
    cb = {}
    for ch in range(4):
        for nm, dt_ in (('Ex', F32), ('u', F32), ('o1', F32)):
            cb[(nm, ch)] = sb("cb_%s_%d" % (nm, ch), [128, 4, 128], dt_)
        for nm in ('Af', 'A0', 'qk', 'qkT', 'PA', 'PB', 'T', 'U', 'Xn', 'vb', 'kbg', 'ktl', 'wT', 'vnew'):
            cb[(nm, ch)] = sb("cb_%s_%d" % (nm, ch), [128, 4, 128], BF16)
    load = {'act': 0.0, 'dve': 0.0}

    def evac(dst, src, reads, writes, scale=None):
        e = 'act' if load['act'] <= load['dve'] else 'dve'
        load[e] += 0.6 if e == 'act' else 0.65
        if e == 'act':
            if scale is None:
                tr.op('act', lambda: nc.scalar.copy(out=dst, in_=src), reads=reads, writes=writes)
            else:
                tr.op('act', lambda: nc.scalar.activation(out=dst, in_=src, func=AF.Copy, scale=scale), reads=reads, writes=writes)
        else:
            if scale is None:
                tr.op('dve', lambda: nc.vector.tensor_copy(out=dst, in_=src), reads=reads, writes=writes)
            else:
                tr.op('dve', lambda: nc.vector.tensor_scalar(out=dst, in0=src, scalar1=scale, scalar2=None, op0=ALU.mult), reads=reads, writes=writes)

    src_of = {'qt': qt_d, 'kt': kt_d, 'ktok': ktok_d, 'vtok': vtok_d}
    order = [list(range(34)), [1, 0] + [33 - i for i in range(32)]]
    I4 = cs['ident4_b'].rearrange("p (g k) -> p g k", g=4)

    def lvlm(kind, l):
        o_ = (kind * 4 + l) * 512
        return cs['lvl'][:, o_:o_ + 512].rearrange("p (g k) -> p g k", g=4)

    def f4(ap2):
        return ap2.rearrange("p (g k) -> p g k", g=4)

    def bc4(ap):
        return ap.unsqueeze(2).to_broadcast([128, 4, 128])

    def issue_loads(step):
        par = step % 2
        for d_ in range(2):
            tg = order[d_][step]
            for nm in ('qt', 'kt', 'ktok', 'vtok'):
                tr.dma('sp', ldb[(d_, par, nm)], src_of[nm][tg], reads=['qkt'], writes=[('ld', d_, par, nm)])

    def prep(d_, tg, par):
        sc = scal[(d_, par)]
        sk = ('scal', d_, par)
        gsl = gb[:, tg, d_ * 8:(d_ + 1) * 8]
        bsl = gb[:, tg, 16 + d_ * 8:16 + (d_ + 1) * 8]
        bk = 4 * d_
        tr.op('pe', lambda: nc.tensor.matmul(bank(bk)[:, 0:8], lhsT=cs['mcum'][:, d_ * 128:(d_ + 1) * 128], rhs=gsl, start=True, stop=True),
              reads=['gb', ('c', 'mcum')], writes=[PB(bk)])
        tr.op('pe', lambda: nc.tensor.matmul(bank(bk)[:, 8:16], lhsT=cs['ones_f'], rhs=gsl, start=True, stop=True),
              reads=['gb', ('c', 'ones_f')], writes=[PB(bk)])
        tr.op('dve', lambda: nc.vector.tensor_copy(out=sc[:, 0:16], in_=bank(bk)[:, 0:16]), reads=[PB(bk)], writes=[sk])
        tr.op('act', lambda: nc.scalar.activation(out=sc[:, 16:32], in_=sc[:, 0:16], func=AF.Exp), reads=[sk], writes=[sk])
        tr.op('dve', lambda: nc.vector.tensor_tensor(out=sc[:, 32:40], in0=sc[:, 8:16], in1=sc[:, 0:8], op=ALU.subtract), reads=[sk], writes=[sk])
        tr.op('act', lambda: nc.scalar.activation(out=sc[:, 40:48], in_=sc[:, 32:40], func=AF.Exp), reads=[sk], writes=[sk])
        tr.op('dve', lambda: nc.vector.tensor_tensor(out=sc[:, 48:56], in0=bsl, in1=sc[:, 16:24], op=ALU.mult), reads=[sk, 'gb'], writes=[sk])

    def chain_step(ch, d_, hg, tg, par, is_x):
        h0 = hg * 4
        b0, b1 = 2 * ch, 2 * ch + 1
        R = lambda n: (n, ch)
        Bf = lambda n: cb[(n, ch)]
        sc = scal[(d_, par)]
        sk = ('scal', d_, par)
        QT, KT, KTOK, VTOK = (ldb[(d_, par, nm)] for nm in ('qt', 'kt', 'ktok', 'vtok'))
        LQ, LK, LKT, LV = (('ld', d_, par, nm) for nm in ('qt', 'kt', 'ktok', 'vtok'))
        bcol = lambda h: gb[:, tg, 16 + d_ * 8 + h:16 + d_ * 8 + h + 1]
        mc = cs['mcum'][:, d_ * 128:(d_ + 1) * 128]
        bigm = cs['bigm'][:, d_ * 128:(d_ + 1) * 128]
        kA = 0 if d_ == 0 else 1
        sidx = lambda i: d_ * 8 + h0 + i
        bk = lambda b, i: bank(b)[:, i * 128:(i + 1) * 128]
        IDB = ('c', 'ident_b')
        for i in range(4):
            gcol = gb[:, tg, d_ * 8 + h0 + i:d_ * 8 + h0 + i + 1]
            tr.op('pe', lambda i=i, gcol=gcol: nc.tensor.matmul(bk(b0, i), lhsT=gcol.to_broadcast([128, 128]), rhs=mc, start=True, stop=False),
                  reads=['gb', ('c', 'mcum')], writes=[PB(b0)])
            tr.op('pe', lambda i=i: nc.tensor.matmul(bk(b0, i), lhsT=cs['ident_b'], rhs=bigm, start=False, stop=True),
                  reads=[('c', 'ident_b'), ('c', 'bigm')], writes=[PB(b0)])
        for i in range(4):
            tr.op('act', lambda i=i: nc.scalar.activation(out=Bf('Ex')[:, i, :], in_=bk(b0, i), func=AF.Exp, bias=sc[:, h0 + i:h0 + i + 1], scale=-1.0),
                  reads=[PB(b0), sk], writes=[R('Ex')])
        load['act'] += 1.3
        yield
        for i in range(4):
            tr.op('pe', lambda i=i: nc.tensor.matmul(bk(b1, i), lhsT=KT[:, h0 + i, :], rhs=KT[:, h0 + i, :], start=True, stop=True),
                  reads=[LK], writes=[PB(b1)])
        for i in range(4):
            tr.op('pe', lambda i=i: nc.tensor.matmul(bk(b0, i), lhsT=QT[:, h0 + i, :], rhs=KT[:, h0 + i, :], start=True, stop=True),
                  reads=[LQ, LK], writes=[PB(b0)])
        for i in range(4):
            tr.op('dve', lambda i=i: nc.vector.scalar_tensor_tensor(out=Bf('Af')[:, i, :], in0=bk(b1, i), scalar=bcol(h0 + i), in1=Bf('Ex')[:, i, :],
                                                                   op0=ALU.mult, op1=ALU.mult),
                  reads=[PB(b1), 'gb', R('Ex')], writes=[R('Af')])
        tr.op('dve', lambda: nc.vector.tensor_tensor(out=Bf('qk'), in0=f4(bank(b0)), in1=Bf('Ex'), op=ALU.mult), reads=[PB(b0), R('Ex')], writes=[R('qk')])
        LV_ = ('c', 'lvl')
        for l in range(1):
            tr.op('dve', lambda l=l: nc.vector.tensor_tensor(out=Bf('A%d' % l), in0=Bf('Af'), in1=lvlm(kA, l), op=ALU.mult),
                  reads=[R('Af'), LV_], writes=[R('A%d' % l)])
        load['dve'] += 1.0 + 0.65 + 0.6
        yield
        for i in range(4):
            tr.op('pe', lambda i=i: nc.tensor.transpose(bankbf(b1)[:, i * 128:(i + 1) * 128], Bf('A0')[:, i, :], cs['ident_b']),
                  reads=[R('A0'), IDB], writes=[PB(b1)])
        for i in range(4):
            tr.op('pe', lambda i=i: nc.tensor.transpose(bankbf(b0)[:, i * 128:(i + 1) * 128], Bf('qk')[:, i, :], cs['ident_b']),
                  reads=[R('qk'), IDB], writes=[PB(b0)])
        evac(Bf('PB'), f4(bankbf(b1)[:, 0:512]), [PB(b1)], [R('PB')])
        evac(Bf('qkT'), f4(bankbf(b0)[:, 0:512]), [PB(b0)], [R('qkT')])
        tr.op('dve', lambda: nc.vector.tensor_tensor(out=Bf('U'), in0=I4, in1=Bf('PB'), op=ALU.subtract), reads=[R('PB'), ('c', 'ident4_b')], writes=[R('U')])
        load['dve'] += 0.4
        yield
        pa_cur = 'A0'
        NLEV = 3
        for lev in range(NLEV):
            for i in range(4):
                tr.op('pe', lambda i=i, pa_cur=pa_cur: nc.tensor.matmul(bk(b0, i), lhsT=Bf('PB')[:, i, :], rhs=Bf(pa_cur)[:, i, :], start=True, stop=True),
                      reads=[R(pa_cur), R('PB')], writes=[PB(b0)])
            if lev < NLEV - 1:
                for i in range(4):
                    tr.op('pe', lambda i=i, pa_cur=pa_cur: nc.tensor.matmul(bk(b1, i), lhsT=Bf(pa_cur)[:, i, :], rhs=Bf('PB')[:, i, :], start=True, stop=True),
                          reads=[R(pa_cur), R('PB')], writes=[PB(b1)])
            evac(Bf('PA'), f4(bank(b0)), [PB(b0)], [R('PA')])
            if lev < NLEV - 1:
                evac(Bf('PB'), f4(bank(b1)), [PB(b1)], [R('PB')])
            pa_cur = 'PA'
            yield
            for i in range(4):
                tr.op('pe', lambda i=i: nc.tensor.matmul(bk(b0, i), lhsT=cs['ident_b'], rhs=Bf('U')[:, i, :], start=True, stop=False),
                      reads=[R('U'), IDB], writes=[PB(b0)])
                tr.op('pe', lambda i=i: nc.tensor.matmul(bk(b0, i), lhsT=Bf('PA')[:, i, :], rhs=Bf('U')[:, i, :], start=False, stop=True),
                      reads=[R('U'), R('PA')], writes=[PB(b0)])
            evac(Bf('U'), f4(bank(b0)), [PB(b0)], [R('U')])
            yield
        for l in (1, 2, 3):
            for i in range(4):
                tr.op('pe', lambda i=i: nc.tensor.transpose(bankbf(b1)[:, i * 128:(i + 1) * 128], Bf('U')[:, i, :], cs['ident_b']),
                      reads=[R('U'), IDB], writes=[PB(b1)])
            for i in range(4):
                tr.op('pe', lambda i=i, l=l: nc.tensor.matmul(bk(b0, i), lhsT=Bf('Af')[:, i, :], rhs=Bf('U')[:, i, :], start=True, stop=True),
                      reads=[R('Af'), R('U')], writes=[PB(b0)])
            evac(Bf('T'), f4(bankbf(b1)[:, 0:512]), [PB(b1)], [R('T')])
            tr.op('dve', lambda l=l: nc.vector.scalar_tensor_tensor(out=Bf('Xn'), in0=f4(bank(b0)), scalar=-1.0, in1=lvlm(1 - kA, l),
                                                                   op0=ALU.mult, op1=ALU.mult),
                  reads=[PB(b0), LV_], writes=[R('Xn')])
            load['dve'] += 0.7
            yield
            for i in range(4):
                tr.op('pe', lambda i=i: nc.tensor.matmul(bk(b0, i), lhsT=cs['ident_b'], rhs=Bf('U')[:, i, :], start=True, stop=False),
                      reads=[R('U'), IDB], writes=[PB(b0)])
                tr.op('pe', lambda i=i: nc.tensor.matmul(bk(b0, i), lhsT=Bf('T')[:, i, :], rhs=Bf('Xn')[:, i, :], start=False, stop=True),
                      reads=[R('T'), R('Xn')], writes=[PB(b0)])
            evac(Bf('U'), f4(bank(b0)), [PB(b0)], [R('U')])
            yield
        tr.op('dve', lambda: nc.vector.tensor_tensor(out=Bf('vb'), in0=VTOK[:, h0:h0 + 4, :], in1=bc4(gb[:, tg, 16 + d_ * 8 + h0:16 + d_ * 8 + h0 + 4]), op=ALU.mult),
              reads=[LV, 'gb'], writes=[R('vb')])
        tr.op('dve', lambda: nc.vector.tensor_tensor(out=Bf('kbg'), in0=KTOK[:, h0:h0 + 4, :], in1=bc4(sc[:, 48 + h0:48 + h0 + 4]), op=ALU.mult),
              reads=[LKT, sk], writes=[R('kbg')])
        tr.op('pool', lambda: nc.gpsimd.tensor_tensor(out=Bf('ktl'), in0=KTOK[:, h0:h0 + 4, :], in1=bc4(sc[:, 40 + h0:40 + h0 + 4]), op=ALU.mult),
              reads=[LKT, sk], writes=[R('ktl')])
        load['dve'] += 0.9
        for i in range(4):
            tr.op('pe', lambda i=i: nc.tensor.matmul(bk(b0, i), lhsT=Bf('U')[:, i, :], rhs=Bf('vb')[:, i, :], start=True, stop=True),
                  reads=[R('U'), R('vb')], writes=[PB(b0)])
        for i in range(4):
            tr.op('pe', lambda i=i: nc.tensor.matmul(bk(b1, i), lhsT=Bf('kbg')[:, i, :], rhs=Bf('U')[:, i, :], start=True, stop=True),
                  reads=[R('U'), R('kbg')], writes=[PB(b1)])
        evac(Bf('u'), f4(bank(b0)), [PB(b0)], [R('u')])
        evac(Bf('wT'), f4(bank(b1)), [PB(b1)], [R('wT')])
        yield
        SK = ('Sb', ch)
        for i in range(4):
            tr.op('pe', lambda i=i: nc.tensor.matmul(bk(b0, i), lhsT=Bf('wT')[:, i, :], rhs=Sb[:, sidx(i), :], start=True, stop=True),
                  reads=[R('wT'), SK], writes=[PB(b0)])
        if is_x:
            for i in range(4):
                tr.op('pe', lambda i=i: nc.tensor.matmul(bk(b1, i), lhsT=QT[:, h0 + i, :], rhs=Sb[:, sidx(i), :], start=True, stop=True),
                      reads=[LQ, SK], writes=[PB(b1)])
        tr.op('dve', lambda: nc.vector.tensor_tensor(out=Bf('vnew'), in0=Bf('u'), in1=f4(bank(b0)), op=ALU.subtract), reads=[PB(b0), R('u')], writes=[R('vnew')])
        load['dve'] += 0.65
        if is_x:
            for i in range(4):
                tr.op('act', lambda i=i: nc.scalar.activation(out=Bf('o1')[:, i, :], in_=bk(b1, i), func=AF.Copy, scale=sc[:, 16 + h0 + i:16 + h0 + i + 1]),
                      reads=[PB(b1), sk], writes=[R('o1')])
            load['act'] += 1.2
        yield
        if is_x:
            for i in range(4):
                tr.op('pe', lambda i=i: nc.tensor.matmul(bk(b1, i), lhsT=Bf('qkT')[:, i, :], rhs=Bf('vnew')[:, i, :], start=True, stop=True),
                      reads=[R('qkT'), R('vnew')], writes=[PB(b1)])
        for i in range(4):
            tr.op('pe', lambda i=i: nc.tensor.matmul(bk(b0, i), lhsT=Bf('ktl')[:, i, :], rhs=Bf('vnew')[:, i, :], start=True, stop=True),
                  reads=[R('ktl'), R('vnew')], writes=[PB(b0)])
        if is_x:
            tr.op('dve', lambda: nc.vector.tensor_tensor(out=f4(ost[(d_, par)][:, hg * 512:(hg + 1) * 512]), in0=f4(bank(b1)), in1=Bf('o1'), op=ALU.add),
                  reads=[PB(b1), R('o1')], writes=[('ost', d_, 0, hg)])
            load['dve'] += 0.65
        for i in range(4):
            tr.op('dve', lambda i=i: nc.vector.scalar_tensor_tensor(out=Sf[:, sidx(i), :], in0=Sf[:, sidx(i), :], scalar=sc[:, 24 + h0 + i:24 + h0 + i + 1],
                                                                   in1=bk(b0, i), op0=ALU.mult, op1=ALU.add),
                  reads=[PB(b0), sk, ('Sf', ch)], writes=[('Sf', ch)])
        load['dve'] += 1.2
        tr.op('act', lambda: nc.scalar.copy(out=Sb[:, d_ * 8 + h0:d_ * 8 + h0 + 4, :], in_=Sf[:, d_ * 8 + h0:d_ * 8 + h0 + 4, :]), reads=[('Sf', ch)], writes=[SK])
        load['act'] += 0.5
        yield

    nsteps = int(os.environ.get("MK_STEPS", "34"))
    issue_loads(0)
    for step in range(nsteps):
        par = step % 2
        if step + 1 < nsteps:
            issue_loads(step + 1)
        gens = []
        for d_ in range(2):
            tg = order[d_][step]
            prep(d_, tg, par)
        for d_ in range(2):
            tg = order[d_][step]
            for hg in range(2):
                gens.append(chain_step(d_ * 2 + hg, d_, hg, tg, par, tg >= 2))
        while gens:
            for g_ in list(gens):
                try:
                    next(g_)
                except StopIteration:
                    gens.remove(g_)
        for d_ in range(2):
            tg = order[d_][step]
            if tg >= 2:
                tr.dma('sp', o_d[d_, tg - 2], ost[(d_, par)], reads=[('ost', d_, 0, 0), ('ost', d_, 0, 1)], writes=['osc'])
    tr.barrier()

    if os.environ.get("MK_STOP", "") == "B2":
        tr.op('dve', lambda: nc.vector.memset(ost[(0, 0)], 0.0), writes=[('ost', 0, 0, 0)])
        for t in range(32):
            tr.dma('sp', out_d[t * 128:(t + 1) * 128, :], ost[(0, 0)], reads=[('ost', 0, 0, 0)], writes=['out'])
        tr.barrier()
        _CACHE['nc'] = nc
        print("instr waits", tr.nwait, "cnt", tr.cnt, "dma", tr.dn)
        return nc
    end_scope()
    new_scope()
    utk = sb("utk", [128, 32, D], BF16)
    for t in range(32):
        tr.dma('sp', utk[:, t, :], utok_d[t], reads=['utok'], writes=[('utk', t)])
    pmb = sb("pmb", [128, NM, 128], BF16)
    tr.dma('sp', pmb, pm_d, writes=['pmb'])
    woutb = sb("woutb", [128, 16, D], BF16)
    for m in range(16):
        tr.dma_w('pool', woutb[:, m, :], wout_d[:, m, :], writes=['woutb'])
    pwb = sb("pwb", [128, 4, 2, 256], BF16)
    for g in range(4):
        tr.dma_w('pool', pwb[:, g, :, :], poolw_d[:, g, :, :], writes=['pwb'])
    gateB = sb("gateB", [128, D])
    tr.dma('sp', gateB, gate_d.partition_broadcast(128), reads=['gate_d'], writes=['gateB'])
    fgB = sb("fgB", [128, D])
    tr.dma('sp', fgB, fg_d.partition_broadcast(128), writes=['fgB'])
    ofb = [sb("ofb%d" % i, [128, D]) for i in range(2)]
    obb = [sb("obb%d" % i, [128, D]) for i in range(2)]
    xcb = [sb("xcb%d" % i, [128, D]) for i in range(3)]
    st3 = sb("st3", [128, 2, 8])
    zdb = [sb("zdb%d" % i, [128, 8, 128], BF16) for i in range(2)]
    zpb = [sb("zpb%d" % i, [128, 8, 128], BF16) for i in range(3)]
    onb2 = [sb("onb%d" % i, [128, 8, 128], BF16) for i in range(2)]
    yTb = [sb("yTb%d" % i, [128, 16, 128], BF16) for i in range(3)]
    dTb2 = [sb("dTb%d" % i, [128, 8, 128], BF16) for i in range(2)]
    resb = sb("resb", [128, D])
    outb = [sb("outb%d" % i, [128, D]) for i in range(1)]
    junk2 = sb("junk2", [128, 128], BF16)
    st2 = sb("st2", [128, 2, 32])
    def c_front(t):
        nonlocal nbk
        p = t % 2
        tr.dma('sp', ofb[p], o_d[0, t], reads=['osc'], writes=[('ofb', p)])
        tr.dma('sp', obb[p], o_d[1, t], reads=['osc'], writes=[('obb', p)])
        tr.dma('sp', zdb[p], zdn_d[t], reads=['zdn'], writes=[('zdb', p)])
        tr.dma('sp', zpb[t % 3], zpl_d[t], reads=['zpl'], writes=[('zpb', t % 3)])
        sk = ('st2', p)
        tr.op('dve', lambda p=p: nc.vector.tensor_tensor(out=ofb[p], in0=ofb[p], in1=obb[p], op=ALU.add), reads=[('ofb', p), ('obb', p)], writes=[('ofb', p)])
        for h in range(8):
            tr.op('act', lambda p=p, h=h: nc.scalar.activation(out=junk2, in_=ofb[p][:, h * 128:(h + 1) * 128], func=AF.Square,
                                                               accum_out=st2[:, p, h:h + 1]),
                  reads=[('ofb', p)], writes=['junk2', sk])
        tr.op('dve', lambda p=p: nc.vector.tensor_scalar(out=st2[:, p, 8:16], in0=st2[:, p, 0:8], scalar1=1.0 / 128, scalar2=EPS, op0=ALU.mult, op1=ALU.add),
              reads=[sk], writes=[sk])
        tr.op('act', lambda p=p: nc.scalar.activation(out=st2[:, p, 16:24], in_=st2[:, p, 8:16], func=AF.Sqrt), reads=[sk], writes=[sk])
        tr.op('dve', lambda p=p: nc.vector.reciprocal(out=st2[:, p, 24:32], in_=st2[:, p, 16:24]), reads=[sk], writes=[sk])
        tr.op('dve', lambda p=p: nc.vector.tensor_tensor(out=onb2[p], in0=ofb[p].rearrange("p (h k) -> p h k", h=8),
                                                        in1=st2[:, p, 24:32].unsqueeze(2).to_broadcast([128, 8, 128]), op=ALU.mult),
              reads=[('ofb', p), sk], writes=[('onb', p)])
    def c_mid(t):
        nonlocal nbk
        p = t % 2
        sk = ('st2', p)
        b = t % 2
        for h in range(8):
            tr.op('pe', lambda b=b, h=h: nc.tensor.transpose(bankbf(b)[:, h * 128:(h + 1) * 128], onb2[p][:, h, :], cs['ident_b']),
                  reads=[('onb', p), ('c', 'ident_b')], writes=[PB(b)])
        yT = yTb[t % 3]
        yk = ('yT', t % 3)
        tr.op('dve', lambda b=b, p=p, yT=yT: nc.vector.scalar_tensor_tensor(out=yT[:, 0:8, :], in0=bankbf(b).rearrange("p (h k) -> p h k", h=8), scalar=dng[:, 0:1],
                                                                           in1=zdb[p], op0=ALU.mult, op1=ALU.mult),
              reads=[PB(b), 'dng', ('zdb', p)], writes=[yk])
        for half in range(2):
            b = 2 + half
            for q in range(4):
                cbk = half * 4 + q
                g = cbk // 2
                lst = psched[t][g]
                for n_, (tin, mid) in enumerate(lst):
                    tr.op('pe', lambda b=b, q=q, cbk=cbk, tin=tin, mid=mid, n_=n_, lst=lst: nc.tensor.matmul(
                        bank(b)[:, q * 128:(q + 1) * 128], lhsT=utk[:, tin, cbk * 128:(cbk + 1) * 128], rhs=pmb[:, mid, :],
                        start=(n_ == 0), stop=(n_ == len(lst) - 1)),
                        reads=[('utk', tin), 'pmb'], writes=[PB(b)])
            e_ = 'act' if half == 0 else 'dve'
            if half == 0:
                tr.op('act', lambda b=b: nc.scalar.copy(out=dTb2[p][:, 0:4, :], in_=f4(bank(b))), reads=[PB(b)], writes=[('dTb', p, 0)])
            else:
                tr.op('dve', lambda b=b: nc.vector.tensor_copy(out=dTb2[p][:, 4:8, :], in_=f4(bank(b))), reads=[PB(b)], writes=[('dTb', p, 1)])
    def c_mid2(t):
        nonlocal nbk
        p = t % 2
        tr.dma('sp', xcb[t % 3], x_d[t * 128:(t + 1) * 128, :], writes=[('xcb', t % 3)])
        yT = yTb[t % 3]
        yk = ('yT', t % 3)
        for half in range(2):
            b = 4 + half
            for q in range(4):
                eb = half * 4 + q
                g = eb // 2
                for cc in range(2):
                    tr.op('pe', lambda b=b, q=q, eb=eb, g=g, cc=cc: nc.tensor.matmul(
                        bank(b)[:, q * 128:(q + 1) * 128], lhsT=pwb[:, g, cc, (eb % 2) * 128:(eb % 2 + 1) * 128], rhs=dTb2[p][:, g * 2 + cc, :],
                        start=(cc == 0), stop=(cc == 1)),
                        reads=['pwb', ('dTb', p, 0), ('dTb', p, 1)], writes=[PB(b)])
            for q in range(4):
                eb = half * 4 + q
                tr.op('dve', lambda b=b, q=q, eb=eb, p=p, yT=yT: nc.vector.scalar_tensor_tensor(
                    out=yT[:, 8 + eb, :], in0=bank(b)[:, q * 128:(q + 1) * 128], scalar=pscale[:, eb:eb + 1], in1=zpb[t % 3][:, eb, :],
                    op0=ALU.mult, op1=ALU.mult),
                    reads=[PB(b), 'pscale', ('zpb', t % 3)], writes=[yk])
    def c_back(t):
        nonlocal nbk
        p = t % 2
        sk = ('st3', p)
        yT = yTb[t % 3]
        yk = ('yT', t % 3)
        for half in range(2):
            b = 6 + half
            for m in range(16):
                tr.op('pe', lambda b=b, m=m, half=half, yT=yT: nc.tensor.matmul(bank(b), lhsT=yT[:, m, :], rhs=woutb[:, m, half * 512:(half + 1) * 512],
                                                                               start=(m == 0), stop=(m == 15)),
                      reads=[yk, 'woutb'], writes=[PB(b)])
            tr.op('dve', lambda b=b, half=half: nc.vector.tensor_tensor(out=resb[:, half * 512:(half + 1) * 512], in0=bank(b), in1=gateB[:, half * 512:(half + 1) * 512],
                                                                        op=ALU.mult),
                  reads=[PB(b), 'gateB'], writes=[('resb', half)])
            tr.op('dve', lambda half=half, p=p: nc.vector.tensor_tensor(out=resb[:, half * 512:(half + 1) * 512], in0=resb[:, half * 512:(half + 1) * 512],
                                                                         in1=xcb[t % 3][:, half * 512:(half + 1) * 512], op=ALU.add),
                  reads=[('resb', half), ('xcb', t % 3)], writes=[('resb', half)])
        sk2 = ('st2b', p)
        tr.op('act', lambda p=p: nc.scalar.activation(out=outb[0], in_=resb, func=AF.Square, accum_out=st3[:, p, 0:1]),
              reads=[('resb', 0), ('resb', 1)], writes=[('outb', 0), sk])
        tr.op('dve', lambda p=p: nc.vector.tensor_scalar(out=st3[:, p, 1:2], in0=st3[:, p, 0:1], scalar1=1.0 / D, scalar2=EPS, op0=ALU.mult, op1=ALU.add),
              reads=[sk], writes=[sk])
        tr.op('act', lambda p=p: nc.scalar.activation(out=st3[:, p, 2:3], in_=st3[:, p, 1:2], func=AF.Sqrt), reads=[sk], writes=[sk])
        tr.op('dve', lambda p=p: nc.vector.reciprocal(out=st3[:, p, 3:4], in_=st3[:, p, 2:3]), reads=[sk], writes=[sk])
        tr.op('dve', lambda p=p: nc.vector.scalar_tensor_tensor(out=outb[0], in0=resb, scalar=st3[:, p, 3:4], in1=fgB, op0=ALU.mult, op1=ALU.mult),
              reads=[('resb', 0), ('resb', 1), sk, 'fgB'], writes=[('outb', 0)])
        tr.dma('act', out_d[t * 128:(t + 1) * 128, :], outb[0], reads=[('outb', 0)], writes=['out'])
    for t in range(32 + 3):
        if t < 32:
            c_front(t)
        if 0 <= t - 1 < 32:
            c_mid(t - 1)
        if 0 <= t - 2 < 32:
            c_mid2(t - 2)
        if t - 3 >= 0:
            c_back(t - 3)
    tr.barrier()
    end_scope()
    _CACHE['nc'] = nc
    print("instr waits", tr.nwait, "cnt", tr.cnt, "dma", tr.dn)
    return nc


def kernel(**inp):
    nc = build()
    C = _consts()
    pmats, _ = _pool_mats()
    f = lambda a: np.ascontiguousarray(np.asarray(a, dtype=np.float32))
    w_mod = f(inp["w_mod"][0]).reshape(8, 128, 3072).transpose(1, 0, 2)
    w_in = f(inp["w_in"][0]).reshape(8, 128, 6176).transpose(1, 0, 2)
    conv_w = f(inp["conv_w"][0]).reshape(5, 24, 128).transpose(2, 1, 0)
    pool_w = f(inp["pool_w"][0]).reshape(4, 2, 128, 256).transpose(2, 0, 1, 3)
    w_out = f(inp["w_out"][0]).reshape(16, 128, 1024).transpose(1, 0, 2)
    shared = {
        "w_mod": np.ascontiguousarray(w_mod), "b_mod": f(inp["b_mod"]).reshape(1, 3072),
        "norm_gain": f(inp["norm_gain"]).reshape(1, D), "w_in": np.ascontiguousarray(w_in),
        "conv_w": np.ascontiguousarray(conv_w), "a_log": f(inp["a_log"]).reshape(1, 16),
        "dt_bias": f(inp["dt_bias"]).reshape(1, 16), "dn_gain": f(inp["dn_norm_gain"]).reshape(128, 1),
        "pool_w": np.ascontiguousarray(pool_w), "pool_scale": np.ascontiguousarray(f(inp["pool_scale"]).reshape(8, 128).T),
        "w_out": np.ascontiguousarray(w_out), "final_gain": f(inp["final_gain"]).reshape(1, D),
        "pmats": np.ascontiguousarray(pmats),
    }
    for k, v in C.items():
        shared["c_" + k] = np.ascontiguousarray(v)
    x = f(inp["x"])
    ctx = f(inp["ctx"])
    c = f(inp["c"])
    cc = f(inp["c_ctx"])
    in_maps = []
    for b in range(8):
        m = dict(shared)
        m["x"] = x[b]
        m["ctx"] = ctx[b]
        m["ccol"] = np.ascontiguousarray(np.concatenate([c[b].reshape(8, 128).T, cc.reshape(8, 128).T], 1))
        in_maps.append(m)
    res = run_bass_kernel_spmd(nc, in_maps, core_ids=list(range(8)))
    _CACHE['res'] = res
    return np.stack([np.asarray(r["out"], dtype=np.float32) for r in res.results], 0)
```

```python
import os
from contextlib import ExitStack
import numpy as np
import ml_dtypes
import concourse.bass as bass
import concourse.mybir as mybir
from concourse.bass_utils import run_bass_kernel_spmd

F32 = mybir.dt.float32
BF16 = mybir.dt.bfloat16
AF = mybir.ActivationFunctionType
ALU = mybir.AluOpType
NPBF = ml_dtypes.bfloat16

D = 1024
L = 4096
LC = 256
NT = 34
EPS = 1e-6
NDS = 40


class TR:
    def __init__(self, nc, needed=None):
        self.nc = nc
        self.needed = needed
        self.wset = set()
        self.idx = {e: 0 for e in ('pe', 'act', 'dve', 'pool')}
        self.eng = {'pe': nc.tensor, 'act': nc.scalar, 'dve': nc.vector, 'pool': nc.gpsimd, 'sp': nc.sync}
        self.sem = {e: nc.alloc_semaphore('s_' + e) for e in ('pe', 'act', 'dve', 'pool')}
        self.cnt = {e: 0 for e in self.sem}
        self.dsem = [nc.alloc_semaphore('d%d' % i) for i in range(NDS)]
        self.dn = 0
        self.wsem = [nc.alloc_semaphore('w%d' % i) for i in range(56)]
        self.wn = 0
        self.waited = {e: {} for e in self.eng}
        self.lastw = {}
        self.reads = {}
        self.nwait = 0

    def _wait(self, e, ev):
        k, sem, val, src = ev
        if self.waited[e].get(k, 0) >= val:
            return
        if src == e and e == 'pe':
            return
        self.eng[e].wait_ge(sem, val)
        self.nwait += 1
        self.waited[e][k] = val
        if self.needed is None and src != 'dma':
            self.wset.add((k, val))

    def _deps(self, e, reads, writes):
        for r in reads:
            if r in self.lastw:
                self._wait(e, self.lastw[r])
        for w in writes:
            if w in self.lastw:
                self._wait(e, self.lastw[w])
            for ev in self.reads.get(w, ()):
                self._wait(e, ev)

    def _record(self, ev, reads, writes):
        for r in reads:
            lst = self.reads.setdefault(r, [])
            lst[:] = [x for x in lst if x[0] != ev[0]]
            lst.append(ev)
        for w in writes:
            self.lastw[w] = ev
            self.reads[w] = []

    def op(self, e, fn, reads=(), writes=()):
        self._deps(e, reads, writes)
        ins = fn()
        self.idx[e] += 1
        if self.needed is None or (e, self.idx[e]) in self.needed:
            self.cnt[e] += 1
            ins.then_inc(self.sem[e], 1)
            ev = (e, self.sem[e], self.cnt[e], e)
        else:
            ev = (e, self.sem[e], self.cnt[e] + 1, e)
        self._record(ev, reads, writes)
        return ev

    def dma(self, q, out, in_, reads=(), writes=()):
        i = self.dn % NDS
        val = 16 * (self.dn // NDS + 1)
        self.dn += 1
        key = 'd%d' % i
        if val > 16:
            self._wait(q, (key, self.dsem[i], val - 16, 'dma'))
        self._deps(q, reads, writes)
        self.eng[q].dma_start(out=out, in_=in_).then_inc(self.dsem[i], 16)
        ev = (key, self.dsem[i], val, 'dma')
        self._record(ev, reads, writes)
        return ev

    def dma_w(self, q, out, in_, reads=(), writes=()):
        i = self.wn
        self.wn += 1
        self._deps(q, reads, writes)
        self.eng[q].dma_start(out=out, in_=in_).then_inc(self.wsem[i], 16)
        ev = ('w%d' % i, self.wsem[i], 16, 'dma')
        self._record(ev, reads, writes)
        return ev

    def barrier(self, engines=('pe', 'act', 'dve', 'pool', 'sp')):
        for e in engines:
            for f in self.sem:
                if f != e and self.cnt[f] > 0:
                    self._wait(e, (f, self.sem[f], self.cnt[f], f))
            for i in range(min(self.dn, NDS)):
                n_used = (self.dn - 1 - i) // NDS + 1
                self._wait(e, ('d%d' % i, self.dsem[i], 16 * n_used, 'dma'))


def _consts():
    c = {}
    idx = np.arange(128)
    low_incl = (idx[None, :] <= idx[:, None]).astype(np.float32)
    up_incl = low_incl.T.copy()

    def bd(s):
        i = idx // s
        return (i[:, None] == i[None, :]).astype(np.float32)
    eye = np.eye(128, dtype=np.float32)
    lowm = [bd(16) * (low_incl - eye)] + [(bd(2 * s) - bd(s)) * low_incl for s in (16, 32, 64)]
    upm = [m.T.copy() for m in lowm]
    rep4 = lambda m: np.tile(m, (1, 4))
    c['ident_f'] = eye
    c['ones_f'] = np.ones((128, 128), np.float32)
    c['mcum'] = np.concatenate([up_incl, low_incl], 1)
    c['ident_b'] = eye.astype(NPBF)
    c['ones_b'] = np.ones((128, 128), NPBF)
    c['ident4_b'] = rep4(eye).astype(NPBF)
    lowm = [bd(16) * (low_incl - eye), (bd(32) - bd(16)) * low_incl, (bd(64) - bd(32)) * low_incl, (bd(128) - bd(64)) * low_incl]
    upm = [m.T.copy() for m in lowm]
    c['lvl'] = np.concatenate([rep4(m) for m in lowm] + [rep4(m) for m in upm], 1).astype(NPBF)
    c['bigm'] = np.concatenate([(1.0 - low_incl) * 30000.0, (1.0 - up_incl) * 30000.0], 1).astype(NPBF)
    return c


def _pool_mats():
    mats = []
    key2id = {}
    sched = [[[] for _ in range(4)] for _ in range(32)]
    for gi, w in enumerate((2, 4, 8, 16)):
        idx = np.arange(64)
        lo = np.clip(idx - w // 2, 0, 64)
        hi = np.clip(idx - w // 2 + w, 0, 64)
        P1 = np.zeros((64, 64), np.float64)
        for i in range(64):
            P1[i, lo[i]:hi[i]] = 1.0 / (hi[i] - lo[i])
        for t in range(32):
            rows_out = [2 * t, 2 * t + 1]
            for tin in range(32):
                rows_in = [2 * tin, 2 * tin + 1]
                blk = np.zeros((128, 128), np.float64)
                for a, rin in enumerate(rows_in):
                    for b_, rout in enumerate(rows_out):
                        blk[a * 64:(a + 1) * 64, b_ * 64:(b_ + 1) * 64] = P1[rout, rin] * P1.T
                if tin == t:
                    blk -= np.eye(128)
                if not np.any(blk):
                    continue
                m = blk.astype(np.float32).astype(NPBF)
                k = m.tobytes()
                if k not in key2id:
                    key2id[k] = len(mats)
                    mats.append(m)
                sched[t][gi].append((tin, key2id[k]))
    return np.stack(mats, 1), sched


_CACHE = {}


def build():
    if 'nc' in _CACHE:
        return _CACHE['nc']
    _CACHE.pop('tr', None)
    _build(None)
    needed = _CACHE['tr'].wset
    _CACHE.pop('nc', None)
    _build(needed)
    return _CACHE['nc']


def _build(needed):
    dbg = os.environ.get("MK_DEBUG", "")
    nc = bass.Bass("TRN2", target_bir_lowering=False)
    tr = TR(nc, needed)
    _CACHE['tr'] = tr
    C = _consts()
    pmats, psched = _pool_mats()
    NM = pmats.shape[1]

    def din(name, shape, dt=F32):
        return nc.dram_tensor(name, list(shape), dt, kind="ExternalInput").ap()

    def dscr(name, shape, dt):
        kind = "ExternalOutput" if name in dbg.split(",") else "Internal"
        return nc.dram_tensor(name, list(shape), dt, kind=kind).ap()

    x_d = din("x", [L, D])
    ctx_d = din("ctx", [LC, D])
    ccol_d = din("ccol", [128, 16])
    wmod_d = din("w_mod", [128, 8, 3072])
    bmod_d = din("b_mod", [1, 3072])
    gain_d = din("norm_gain", [1, D])
    win_d = din("w_in", [128, 8, 6176])
    convw_d = din("conv_w", [128, 24, 5])
    alog_d = din("a_log", [1, 16])
    dtb_d = din("dt_bias", [1, 16])
    dng_d = din("dn_gain", [128, 1])
    poolw_d = din("pool_w", [128, 4, 2, 256])
    pscale_d = din("pool_scale", [128, 8])
    wout_d = din("w_out", [128, 16, D])
    fg_d = din("final_gain", [1, D])
    cf_d = {k: din("c_" + k, v.shape, F32 if v.dtype == np.float32 else BF16) for k, v in C.items()}
    pm_d = din("pmats", [128, NM, 128], BF16)
    out_d = nc.dram_tensor("out", [L, D], F32, kind="ExternalOutput").ap()

    rawx_d = dscr("rawx", [24, 128, L + 4], F32)
    rawc_d = dscr("rawc", [24, 128, LC + 4], F32)
    qt_d = dscr("qt", [NT, 128, 8, 128], BF16)
    kt_d = dscr("kt", [NT, 128, 8, 128], BF16)
    ktok_d = dscr("ktok", [NT, 128, 8, 128], BF16)
    vtok_d = dscr("vtok", [NT, 128, 8, 128], BF16)
    zdn_d = dscr("zdn", [32, 128, 8, 128], BF16)
    zpl_d = dscr("zpl", [32, 128, 8, 128], BF16)
    utok_d = dscr("utok", [32, 128, D], BF16)
    o_d = dscr("osc", [2, 32, 128, D], F32)
    gb_d = dscr("gbdbg", [128, NT, 32], F32)

    scopes = [ExitStack()]

    def sb(name, shape, dt=F32):
        h = scopes[-1].enter_context(nc.sbuf_tensor("sb_" + name, list(shape), dt))
        return h.ap() if hasattr(h, "ap") and callable(h.ap) else h

    def new_scope():
        scopes.append(ExitStack())

    def end_scope():
        scopes.pop().close()

    psum = nc.alloc_psum_tensor("ps", [128, 8 * 512], F32).ap()

    def bank(b, n=512):
        return psum[:, b * 512:b * 512 + n]

    def bankbf(b):
        return psum[:, b * 512:(b + 1) * 512].bitcast(BF16)

    def PB(b):
        return ('ps', b)

    eng = tr.eng

    cs = {}
    for k, v in C.items():
        cs[k] = sb("k_" + k, v.shape, F32 if v.dtype == np.float32 else BF16)
        tr.dma('sp', cs[k], cf_d[k], writes=[('c', k)])
    gb = sb("gb", [128, NT, 32])
    small = sb("small", [128, 64])
    epsc = small[:, 0:1]
    tr.op('dve', lambda: nc.vector.memset(epsc, EPS), writes=['epsc'])
    convw = sb("convw", [128, 24, 5])
    tr.dma('sp', convw, convw_d, writes=['convw'])
    dng = sb("dng", [128, 1])
    tr.dma('sp', dng, dng_d, writes=['dng'])
    pscale = sb("pscale", [128, 8])
    tr.dma('sp', pscale, pscale_d, writes=['pscale'])
    adt = sb("adt", [128, 32])
    tr.dma('sp', adt[:, 0:16], alog_d.partition_broadcast(128), writes=['adt'])
    tr.dma('sp', adt[:, 16:32], dtb_d.partition_broadcast(128), writes=['adt'])
    nexpa = sb("nexpa", [128, 16])
    tr.op('act', lambda: nc.scalar.activation(out=nexpa, in_=adt[:, 0:16], func=AF.Exp), reads=['adt'], writes=['nexpa'])
    tr.op('dve', lambda: nc.vector.tensor_scalar(out=nexpa, in0=nexpa, scalar1=-1.0, scalar2=None, op0=ALU.mult),
          reads=['nexpa'], writes=['nexpa'])
    gate_d = dscr("gate_scr", [1, D], F32)
    new_scope()
    bcast = sb("bcast", [128, 5, D])
    winb = sb("winb", [128, 8, 6176], BF16)
    for hh in range(4):
        for k in range(8):
            tr.dma_w('pool', winb[:, k, hh * 1544:(hh + 1) * 1544], win_d[:, k, hh * 1544:(hh + 1) * 1544], writes=[('winb', hh, k)])

    new_scope()
    wmr = [sb("wmr%d" % i, [128, 8, 512]) for i in range(2)]
    ccol = sb("ccol", [128, 16])
    tr.dma('sp', ccol, ccol_d, writes=['ccol'])
    rows = sb("rows", [1, 2, 3072])
    brow = sb("brow", [1, 3072])
    grow = sb("grow", [1, D])
    tr.dma('sp', brow, bmod_d, writes=['brow'])
    tr.dma('sp', grow, gain_d, writes=['grow'])
    scol = sb("scol", [128, 16])
    tr.op('act', lambda: nc.scalar.activation(out=scol, in_=ccol, func=AF.Silu), reads=['ccol'], writes=['scol'])
    for cg in range(6):
        with nc.allow_non_contiguous_dma(reason="w_mod column group"):
            tr.dma('sp', wmr[cg % 2], wmod_d[:, :, cg * 512:(cg + 1) * 512], writes=[('wmr', cg % 2)])
        for which in range(2):
            b = (which * 6 + cg) % 8
            for k in range(8):
                tr.op('pe', lambda k=k, b=b, which=which: nc.tensor.matmul(bank(b)[0:1, :], lhsT=scol[:, which * 8 + k:which * 8 + k + 1],
                                                                           rhs=wmr[cg % 2][:, k, :], start=(k == 0), stop=(k == 7)),
                      reads=['scol', ('wmr', cg % 2)], writes=[PB(b)])
            tr.op('dve', lambda b=b: nc.vector.tensor_tensor(out=rows[0:1, which, cg * 512:(cg + 1) * 512], in0=bank(b)[0:1, :],
                                                             in1=brow[0:1, cg * 512:(cg + 1) * 512], op=ALU.add),
                  reads=[PB(b), 'brow'], writes=['rows'])
    for which in range(2):
        tr.op('dve', lambda which=which: nc.vector.scalar_tensor_tensor(out=rows[0:1, which, 1024:2048], in0=rows[0:1, which, 1024:2048],
                                                                        scalar=1.0, in1=grow[0:1, :], op0=ALU.add, op1=ALU.mult),
              reads=['rows', 'grow'], writes=['rows'])
    tr.dma('sp', gate_d, rows[0:1, 0, 2048:3072], reads=['rows'], writes=['gate_d'])
    bl = [(0, 0, 1024), (1, 0, 0), (3, 1, 1024), (4, 1, 0)]
    ones_row = cs['ones_f'][0:1, :]
    nb = 0
    for slot, which, off in bl:
        for half in range(2):
            b = nb % 8
            nb += 1
            tr.op('pe', lambda b=b, which=which, off=off, half=half: nc.tensor.matmul(
                bank(b), lhsT=ones_row, rhs=rows[0:1, which, off + half * 512: off + (half + 1) * 512], start=True, stop=True),
                reads=['rows', ('c', 'ones_f')], writes=[PB(b)])
            tr.op('act', lambda b=b, slot=slot, half=half: nc.scalar.copy(out=bcast[:, slot, half * 512:(half + 1) * 512], in_=bank(b)),
                  reads=[PB(b)], writes=[('bcast', slot)])
    tr.barrier()
    end_scope()
    new_scope()

    ztile = sb("ztile", [128, 24, 2])
    tr.op('pool', lambda: nc.gpsimd.memset(ztile, 0.0), writes=['ztile'])
    with nc.allow_non_contiguous_dma(reason="halo zero fill"):
        for rd, ll in ((rawx_d, L), (rawc_d, LC)):
            tr.dma('sp', rd[:, :, 0:2].rearrange("b p t -> p b t"), ztile, reads=['ztile'], writes=['rawhalo'])
            tr.dma('sp', rd[:, :, ll + 2:ll + 4].rearrange("b p t -> p b t"), ztile, reads=['ztile'], writes=['rawhalo'])

    xb = [sb("xb%d" % i, [128, D]) for i in range(2)]
    junk = sb("junk", [128, D], BF16)
    h1 = [sb("h1_%d" % i, [128, D]) for i in range(2)]
    h2 = [sb("h2_%d" % i, [128, D], BF16) for i in range(2)]
    hnT = [sb("hnT%d" % i, [128, 8, 512], BF16) for i in range(2)]
    st32 = [sb("st32_%d" % i, [128, 512]) for i in range(6)]
    st16 = [sb("st16_%d" % i, [128, 512], BF16) for i in range(5)]
    stat = sb("stat", [128, 2, 8])
    abt = sb("abt", [128, 2, 64])

    h2x = [sb("h2x_%d" % i, [128, D], BF16) for i in range(6)]
    h2all = h2 + h2x
    seqs = [(ctx_d, rawc_d, 2, 0, False), (x_d, rawx_d, 32, 2, True)]
    mts = []
    for src_d, raw_d, ntiles, tbase, isx in seqs:
        for mt in range((ntiles + 3) // 4):
            mts.append((src_d, raw_d, min(4, ntiles - 4 * mt), tbase, isx, mt))
    st = {'nxt': 0, 'nbk': 0, 'n32': 0, 'n16': 0}
    def wq(c0, c1):
        return [('winb', hh, k) for hh in range(c0 // 1544, (c1 - 1) // 1544 + 1) for k in range(8)]

    def a_front(kk):
        src_d, raw_d, nt, tbase, isx, mt = mts[kk]
        gmslot, shslot = (0, 1) if isx else (3, 4)
        for i in range(nt):
            tl = 4 * mt + i
            p = st['nxt'] % 2
            st['nxt'] += 1
            hi = (kk % 2) * 4 + i
            xt = xb[p]
            tr.dma('sp', xt, src_d[tl * 128:(tl + 1) * 128, :], writes=[('xb', p)])
            tr.op('act', lambda xt=xt, p=p: nc.scalar.activation(out=junk, in_=xt, func=AF.Square, accum_out=stat[:, p, 0:1]),
                  reads=[('xb', p)], writes=['junk', ('stat', p)])
            tr.op('dve', lambda p=p: nc.vector.tensor_scalar(out=stat[:, p, 1:2], in0=stat[:, p, 0:1], scalar1=1.0 / D, scalar2=EPS,
                                                              op0=ALU.mult, op1=ALU.add), reads=[('stat', p)], writes=[('stat', p)])
            tr.op('act', lambda p=p: nc.scalar.activation(out=stat[:, p, 2:3], in_=stat[:, p, 1:2], func=AF.Sqrt),
                  reads=[('stat', p)], writes=[('stat', p)])
            tr.op('dve', lambda p=p: nc.vector.reciprocal(out=stat[:, p, 3:4], in_=stat[:, p, 2:3]), reads=[('stat', p)], writes=[('stat', p)])
            tr.op('dve', lambda p=p, xt=xt: nc.vector.scalar_tensor_tensor(out=h1[p], in0=xt, scalar=stat[:, p, 3:4], in1=bcast[:, gmslot, :],
                                                                           op0=ALU.mult, op1=ALU.mult),
                  reads=[('xb', p), ('stat', p), ('bcast', gmslot)], writes=[('h1', p)])
            tr.op('pool', lambda p=p, hi=hi: nc.gpsimd.tensor_tensor(out=h2all[hi], in0=h1[p], in1=bcast[:, shslot, :], op=ALU.add),
                  reads=[('h1', p), ('bcast', shslot)], writes=[('h2', hi)])

    def a_trans(kk):
        src_d, raw_d, nt, tbase, isx, mt = mts[kk]
        hT = hnT[kk % 2]
        hkey = ('hnT', kk % 2)
        for i in range(nt):
            hi = (kk % 2) * 4 + i
            b = st['nbk'] % 8
            st['nbk'] += 1
            for k in range(8):
                tr.op('pe', lambda k=k, b=b, hi=hi: nc.tensor.transpose(bankbf(b)[:, k * 128:(k + 1) * 128], h2all[hi][:, k * 128:(k + 1) * 128], cs['ident_b']),
                      reads=[('h2', hi), ('c', 'ident_b')], writes=[PB(b)])
            tr.op('act', lambda b=b, i=i, hT=hT: nc.scalar.copy(out=hT[:, :, i * 128:(i + 1) * 128],
                                                                in_=bankbf(b).rearrange("p (k t) -> p k t", k=8)),
                  reads=[PB(b)], writes=[hkey])

    def a_proj(kk, part):
        src_d, raw_d, nt, tbase, isx, mt = mts[kk]
        N = nt * 128
        hT = hnT[kk % 2]
        hkey = ('hnT', kk % 2)
        blocks = [(j, j * 128, 'raw') for j in range(24)]
        if isx:
            blocks += [(j, 3104 + j * 128, 'zdn') for j in range(8)] + [(j, 5152 + j * 128, 'zpl') for j in range(8)]
        half = len(blocks) // 2
        sel = blocks[:half] if part == 0 else blocks[half:]
        for bi, (j, col0, kind) in enumerate(sel):
            b = st['nbk'] % 8
            st['nbk'] += 1
            for k in range(8):
                tr.op('pe', lambda k=k, b=b, col0=col0: nc.tensor.matmul(bank(b, N), lhsT=winb[:, k, col0:col0 + 128], rhs=hT[:, k, 0:N],
                                                                        start=(k == 0), stop=(k == 7)),
                      reads=[hkey] + wq(col0, col0 + 128), writes=[PB(b)])
            e = 'act' if (bi % 2 == 0 or kind != 'raw') else 'dve'
            if kind == 'raw':
                s_ = st['n32'] % 6
                st['n32'] += 1
                if e == 'act':
                    tr.op('act', lambda b=b, s_=s_: nc.scalar.copy(out=st32[s_][:, 0:N], in_=bank(b, N)), reads=[PB(b)], writes=[('st32', s_)])
                else:
                    tr.op('dve', lambda b=b, s_=s_: nc.vector.tensor_copy(out=st32[s_][:, 0:N], in_=bank(b, N)), reads=[PB(b)], writes=[('st32', s_)])
                tr.dma('sp', raw_d[j, :, 2 + mt * 512: 2 + mt * 512 + N], st32[s_][:, 0:N], reads=[('st32', s_)], writes=['raw'])
            else:
                s_ = st['n16'] % 5
                st['n16'] += 1
                tr.op('act', lambda b=b, s_=s_: nc.scalar.activation(out=st16[s_][:, 0:N], in_=bank(b, N), func=AF.Silu),
                      reads=[PB(b)], writes=[('st16', s_)])
                zd = zdn_d if kind == 'zdn' else zpl_d
                tr.dma('sp', zd[4 * mt:4 * mt + nt, :, j, :].rearrange("t p k -> p t k"),
                       st16[s_][:, 0:N].rearrange("p (t k) -> p t k", t=nt), reads=[('st16', s_)], writes=[kind])
        if part == 0:
            return
        for i in range(nt):
            tl = 4 * mt + i
            if isx:
                for cg in range(2):
                    b = st['nbk'] % 8
                    st['nbk'] += 1
                    for k in range(8):
                        tr.op('pe', lambda k=k, b=b, cg=cg, i=i: nc.tensor.matmul(bank(b), lhsT=hT[:, k, i * 128:(i + 1) * 128],
                                                                                  rhs=winb[:, k, 4128 + cg * 512: 4128 + (cg + 1) * 512],
                                                                                  start=(k == 0), stop=(k == 7)),
                              reads=[hkey] + wq(4128, 5152), writes=[PB(b)])
                    s_ = st['n16'] % 5
                    st['n16'] += 1
                    tr.op('dve', lambda b=b, s_=s_: nc.vector.tensor_copy(out=st16[s_], in_=bank(b)), reads=[PB(b)], writes=[('st16', s_)])
                    tr.dma('sp', utok_d[tl, :, cg * 512:(cg + 1) * 512], st16[s_], reads=[('st16', s_)], writes=['utok'])
            b = st['nbk'] % 8
            st['nbk'] += 1
            for k in range(8):
                tr.op('pe', lambda k=k, b=b, i=i: nc.tensor.matmul(bank(b, 32), lhsT=hT[:, k, i * 128:(i + 1) * 128], rhs=winb[:, k, 3072:3104],
                                                                   start=(k == 0), stop=(k == 7)),
                      reads=[hkey] + wq(3072, 3104), writes=[PB(b)])
            q = tl % 2
            gt = tbase + tl
            ak = ('abt', q)
            tr.op('dve', lambda b=b, q=q: nc.vector.tensor_tensor(out=abt[:, q, 0:16], in0=bank(b, 32)[:, 0:16], in1=adt[:, 16:32], op=ALU.add),
                  reads=[PB(b), 'adt'], writes=[ak])
            tr.op('act', lambda q=q: nc.scalar.activation(out=abt[:, q, 16:32], in_=abt[:, q, 0:16], func=AF.Exp), reads=[ak], writes=[ak])
            tr.op('act', lambda q=q: nc.scalar.activation(out=abt[:, q, 32:48], in_=abt[:, q, 16:32], func=AF.Ln, bias=1.0, scale=1.0),
                  reads=[ak], writes=[ak])
            tr.op('dve', lambda q=q, gt=gt: nc.vector.tensor_tensor(out=gb[:, gt, 0:16], in0=abt[:, q, 32:48], in1=nexpa, op=ALU.mult),
                  reads=[ak, 'nexpa'], writes=['gb'])
            tr.op('act', lambda b=b, q=q: nc.scalar.activation(out=abt[:, q, 48:64], in_=bank(b, 32)[:, 16:32], func=AF.Exp, scale=-1.0),
                  reads=[PB(b)], writes=[ak])
            tr.op('dve', lambda q=q: nc.vector.tensor_scalar(out=abt[:, q, 48:64], in0=abt[:, q, 48:64], scalar1=1.0, scalar2=None, op0=ALU.add),
                  reads=[ak], writes=[ak])
            tr.op('dve', lambda q=q, gt=gt: nc.vector.reciprocal(out=gb[:, gt, 16:32], in_=abt[:, q, 48:64]), reads=[ak], writes=['gb'])

    a_front(0)
    a_trans(0)
    for kk in range(len(mts)):
        if kk + 1 < len(mts):
            a_front(kk + 1)
        a_proj(kk, 0)
        if kk + 1 < len(mts):
            a_trans(kk + 1)
        a_proj(kk, 1)
    nbk = st['nbk']
    if 'gbdbg' in dbg:
        tr.dma('sp', gb_d, gb, reads=['gb'], writes=['gbd'])
    tr.barrier()
    _CACHE['after_A'] = True

    if os.environ.get("MK_STOP", "") == "A":
        tr.op('dve', lambda: nc.vector.memset(st32[0], 0.0), writes=[('st32', 0)])
        for t in range(32):
            for hf in range(2):
                tr.dma('sp', out_d[t * 128:(t + 1) * 128, hf * 512:(hf + 1) * 512], st32[0], reads=[('st32', 0)], writes=['out'])
        tr.barrier()
        _CACHE['nc'] = nc
        print("instr waits", tr.nwait, "cnt", tr.cnt, "dma", tr.dn)
        return nc
    end_scope()
    end_scope()
    new_scope()
    rawb = [sb("rawb%d" % i, [128, 516]) for i in range(4)]
    accb = [sb("accb%d" % i, [128, 512]) for i in range(4)]
    sil = [sb("sil%d" % i, [128, 512]) for i in range(8)]
    sqb = [sb("sqb%d" % i, [128, 512], BF16) for i in range(2)]
    sdb = [sb("sdb%d" % i, [128, 512]) for i in range(8)]
    nrm = [sb("nrm%d" % i, [128, 512], BF16) for i in range(6)]
    vnr = [sb("vnr%d" % i, [128, 512], BF16) for i in range(8)]
    tkb = [sb("tkb%d" % i, [128, 512], BF16) for i in range(6)]
    dg = sb("dg", [128, 72, 128])
    for blk in range(24):
        for j in range(3):
            if (blk * 3 + j) % 2 == 0:
                tr.op('act', lambda blk=blk, j=j: nc.scalar.activation(out=dg[:, blk * 3 + j, :], in_=cs['ident_f'], func=AF.Copy, scale=convw[:, blk, j:j + 1]),
                      reads=['convw', ('c', 'ident_f')], writes=['dg'])
            else:
                tr.op('dve', lambda blk=blk, j=j: nc.vector.tensor_scalar(out=dg[:, blk * 3 + j, :], in0=cs['ident_f'], scalar1=convw[:, blk, j:j + 1],
                                                                         scalar2=None, op0=ALU.mult),
                      reads=['convw', ('c', 'ident_f')], writes=['dg'])
    items = []
    for raw_d, ntiles, tbase in ((rawc_d, 2, 0), (rawx_d, 32, 2)):
        for mt in range((ntiles + 3) // 4):
            nt = min(4, ntiles - 4 * mt)
            for blk in range(24):
                items.append((raw_d, mt, nt, tbase + 4 * mt, blk))
    cnt = {'nn': 0, 'ntk': 0, 'nbk': nbk}

    def b1_front(ix):
        raw_d, mt, nt, gt0, blk = items[ix]
        N = nt * 128
        r_ = ix % 4
        a_ = ix % 4
        tr.dma('sp', rawb[r_][:, 0:N + 4], raw_d[blk, :, mt * 512: mt * 512 + N + 4], reads=['raw', 'rawhalo'], writes=[('rawb', r_)])
        b = ix % 2
        for j in range(3):
            tr.op('pe', lambda j=j, b=b: nc.tensor.matmul(bank(b, N), lhsT=dg[:, blk * 3 + j, :], rhs=rawb[r_][:, j:j + N], start=(j == 0), stop=(j == 2)),
                  reads=[('rawb', r_), 'dg'], writes=[PB(b)])
        acc = accb[a_]
        tr.op('dve', lambda: nc.vector.scalar_tensor_tensor(out=acc[:, 0:N], in0=rawb[r_][:, 3:3 + N], scalar=convw[:, blk, 3:4], in1=bank(b, N),
                                                           op0=ALU.mult, op1=ALU.add),
              reads=[('rawb', r_), 'convw', PB(b)], writes=[('accb', a_)])
        tr.op('dve', lambda: nc.vector.scalar_tensor_tensor(out=acc[:, 0:N], in0=rawb[r_][:, 4:4 + N], scalar=convw[:, blk, 4:5], in1=acc[:, 0:N],
                                                           op0=ALU.mult, op1=ALU.add),
              reads=[('rawb', r_), 'convw', ('accb', a_)], writes=[('accb', a_)])

    NB8 = 8
    memo = {}

    def b1_s1(ixs):
        for ix in ixs:
            raw_d, mt, nt, gt0, blk = items[ix]
            N = nt * 128
            a_ = ix % 4
            s_ = ix % NB8
            acc = accb[a_]
            if blk < 16:
                tr.op('act', lambda acc=acc, s_=s_, N=N: nc.scalar.activation(out=sil[s_][:, 0:N], in_=acc[:, 0:N], func=AF.Silu),
                      reads=[('accb', a_)], writes=[('sil', s_)])
                q_ = ix % 2
                tr.op('pool', lambda s_=s_, N=N, q_=q_: nc.gpsimd.tensor_tensor(out=sqb[q_][:, 0:N], in0=sil[s_][:, 0:N], in1=sil[s_][:, 0:N], op=ALU.mult),
                      reads=[('sil', s_)], writes=[('sqb', q_)])
                b2_ = 2 + (ix % 4)
                tr.op('pe', lambda b2_=b2_, N=N, q_=q_: nc.tensor.matmul(bank(b2_, N), lhsT=cs['ones_b'], rhs=sqb[q_][:, 0:N], start=True, stop=True),
                      reads=[('sqb', q_), ('c', 'ones_b')], writes=[PB(b2_)])
                memo[('ss', ix)] = b2_
            else:
                tr.op('act', lambda acc=acc, s_=s_, N=N: nc.scalar.activation(out=vnr[s_][:, 0:N], in_=acc[:, 0:N], func=AF.Silu),
                      reads=[('accb', a_)], writes=[('vnr', s_)])

    def b1_s2(ixs):
        todo = [(ix, memo.pop(('ss', ix))) for ix in ixs if ('ss', ix) in memo]
        for ix, b2_ in todo:
            N = items[ix][2] * 128
            s_ = ix % NB8
            tr.op('act', lambda b2_=b2_, s_=s_, N=N: nc.scalar.activation(out=sdb[s_][:, 0:N], in_=bank(b2_, N), func=AF.Ln, bias=epsc, scale=1.0),
                  reads=[PB(b2_), 'epsc'], writes=[('sdb', s_)])
        for ix, b2_ in todo:
            N = items[ix][2] * 128
            s_ = ix % NB8
            tr.op('act', lambda s_=s_, N=N: nc.scalar.activation(out=sdb[s_][:, 0:N], in_=sdb[s_][:, 0:N], func=AF.Exp, scale=-0.5),
                  reads=[('sdb', s_)], writes=[('sdb', s_)])

    def b1_s3(ix):
        raw_d, mt, nt, gt0, blk = items[ix][:5]
        N = nt * 128
        s_ = ix % NB8
        head = blk % 8
        if blk < 16:
            n_ = cnt['nn'] % 6
            cnt['nn'] += 1
            cst = (128.0 ** -0.5) if blk < 8 else 1.0
            tr.op('dve', lambda: nc.vector.scalar_tensor_tensor(out=nrm[n_][:, 0:N], in0=sil[s_][:, 0:N], scalar=cst, in1=sdb[s_][:, 0:N],
                                                               op0=ALU.mult, op1=ALU.mult),
                  reads=[('sil', s_), ('sdb', s_)], writes=[('nrm', n_)])
            dst = qt_d if blk < 8 else kt_d
            memo.setdefault(('st', ix), []).append((dst[gt0:gt0 + nt, :, head, :].rearrange("t p k -> p t k"),
                                                    nrm[n_][:, 0:N].rearrange("p (t k) -> p t k", t=nt), ('nrm', n_)))
            srcb, skey = nrm[n_], ('nrm', n_)
        else:
            srcb, skey = vnr[s_], ('vnr', s_)
        if blk >= 8:
            b = 6 + (ix % 2)
            for i in range(nt):
                tr.op('pe', lambda i=i: nc.tensor.transpose(bankbf(b)[:, i * 128:(i + 1) * 128], srcb[:, i * 128:(i + 1) * 128], cs['ident_b']),
                      reads=[skey, ('c', 'ident_b')], writes=[PB(b)])
            memo[('tp', ix)] = b

    def b1_s4(ix):
        raw_d, mt, nt, gt0, blk = items[ix][:5]
        if blk < 8:
            return
        N = nt * 128
        head = blk % 8
        b = memo.pop(('tp', ix))
        k_ = cnt['ntk'] % 6
        cnt['ntk'] += 1
        if blk < 16:
            tr.op('dve', lambda: nc.vector.tensor_copy(out=tkb[k_][:, 0:N], in_=bankbf(b)[:, 0:N]), reads=[PB(b)], writes=[('tkb', k_)])
        else:
            tr.op('act', lambda: nc.scalar.copy(out=tkb[k_][:, 0:N], in_=bankbf(b)[:, 0:N]), reads=[PB(b)], writes=[('tkb', k_)])
        dst = ktok_d if blk < 16 else vtok_d
        memo.setdefault(('st', ix), []).append((dst[gt0:gt0 + nt, :, head, :].rearrange("t p k -> p t k"),
                                                tkb[k_][:, 0:N].rearrange("p (t k) -> p t k", t=nt), ('tkb', k_)))

    def b1_s5(ix):
        for dst_ap, src_ap, key in memo.pop(('st', ix), []):
            tr.dma('sp', dst_ap, src_ap, reads=[key], writes=['qkt'])

    nit = len(items)
    for n in range(nit + 12):
        if n < nit:
            b1_front(n)
        m1 = n - 2
        if m1 % 2 == 1 and 0 <= m1 < nit:
            b1_s1([m1 - 1, m1])
        m2 = n - 4
        if m2 % 2 == 1 and 0 <= m2 < nit:
            b1_s2([m2 - 1, m2])
        if 0 <= n - 7 < nit:
            b1_s3(n - 7)
        if 0 <= n - 8 < nit:
            b1_s4(n - 8)
        if 0 <= n - 10 < nit:
            b1_s5(n - 10)
    nbk = cnt['nbk']
    tr.barrier()

    if os.environ.get("MK_STOP", "") == "B1":
        tr.op('dve', lambda: nc.vector.memset(accb[0], 0.0), writes=[('accb', 0)])
        for t in range(32):
            for hf in range(2):
                tr.dma('sp', out_d[t * 128:(t + 1) * 128, hf * 512:(hf + 1) * 512], accb[0], reads=[('accb', 0)], writes=['out'])
        tr.barrier()
        _CACHE['nc'] = nc
        print("instr waits", tr.nwait, "cnt", tr.cnt, "dma", tr.dn)
        return nc
    end_scope()
    new_scope()
    Sf = sb("Sf", [128, 16, 128])
    Sb = sb("Sb", [128, 16, 128], BF16)
    tr.op('dve', lambda: nc.vector.memset(Sf, 0.0), writes=[('Sf', c_) for c_ in range(4)])
    tr.op('pool', lambda: nc.gpsimd.memset(Sb, 0.0), writes=[('Sb', c_) for c_ in range(4)])
    ldb = {}
    for d_ in range(2):
        for p_ in range(2):
            for nm in ('qt', 'kt', 'ktok', 'vtok'):
                ldb[(d_, p_, nm)] = sb("ld_%s_%d_%d" % (nm, d_, p_), [128, 8, 128], BF16)
    scal = {(d_, p_): sb("scal_%d_%d" % (d_, p_), [128, 64]) for d_ in range(2) for p_ in range(2)}
    ost1 = {d_: sb("ost_%d" % d_, [128, D]) for d_ in range(2)}
    ost = {(d_, p_): ost1[d_] for d_ in range(2) for p_ in range(2)}
    cb = {}
    for ch in range(4):
        for nm, dt_ in (('Ex', F32), ('u', F32), ('o1', F32)):
            cb[(nm, ch)] = sb("cb_%s_%d" % (nm, ch), [128, 4, 128], dt_)
        for nm in ('Af', 'A0', 'qk', 'qkT', 'PA', 'PB', 'T', 'U', 'Xn', 'vb', 'kbg', 'ktl', 'wT', 'vnew'):
            cb[(nm, ch)] = sb("cb_%s_%d" % (nm, ch), [128, 4, 128], BF16)
    load = {'act': 0.0, 'dve': 0.0}

    def evac(dst, src, reads, writes, scale=None):
        e = 'act' if load['act'] <= load['dve'] else 'dve'
        load[e] += 0.6 if e == 'act' else 0.65
        if e == 'act':
            if scale is None:
                tr.op('act', lambda: nc.scalar.copy(out=dst, in_=src), reads=reads, writes=writes)
            else:
                tr.op('act', lambda: nc.scalar.activation(out=dst, in_=src, func=AF.Copy, scale=scale), reads=reads, writes=writes)
        else:
            if scale is None:
                tr.op('dve', lambda: nc.vector.tensor_copy(out=dst, in_=src), reads=reads, writes=writes)
            else:
                tr.op('dve', lambda: nc.vector.tensor_scalar(out=dst, in0=src, scalar1=scale, scalar2=None, op0=ALU.mult), reads=reads, writes=writes)

    src_of = {'qt': qt_d, 'kt': kt_d, 'ktok': ktok_d, 'vtok': vtok_d}
    order = [list(range(34)), [1, 0] + [33 - i for i in range(32)]]
    I4 = cs['ident4_b'].rearrange("p (g k) -> p g k", g=4)

    def lvlm(kind, l):
        o_ = (kind * 4 + l) * 512
        return cs['lvl'][:, o_:o_ + 512].rearrange("p (g k) -> p g k", g=4)

    def f4(ap2):
        return ap2.rearrange("p (g k) -> p g k", g=4)

    def bc4(ap):
        return ap.unsqueeze(2).to_broadcast([128, 4, 128])

    def issue_loads(step):
        par = step % 2
        for d_ in range(2):
            tg = order[d_][step]
            for nm in ('qt', 'kt', 'ktok', 'vtok'):
                tr.dma('sp', ldb[(d_, par, nm)], src_of[nm][tg], reads=['qkt'], writes=[('ld', d_, par, nm)])

    def prep(d_, tg, par):
        sc = scal[(d_, par)]
        sk = ('scal', d_, par)
        gsl = gb[:, tg, d_ * 8:(d_ + 1) * 8]
        bsl = gb[:, tg, 16 + d_ * 8:16 + (d_ + 1) * 8]
        bk = 4 * d_
        tr.op('pe', lambda: nc.tensor.matmul(bank(bk)[:, 0:8], lhsT=cs['mcum'][:, d_ * 128:(d_ + 1) * 128], rhs=gsl, start=True, stop=True),
              reads=['gb', ('c', 'mcum')], writes=[PB(bk)])
        tr.op('pe', lambda: nc.tensor.matmul(bank(bk)[:, 8:16], lhsT=cs['ones_f'], rhs=gsl, start=True, stop=True),
              reads=['gb', ('c', 'ones_f')], writes=[PB(bk)])
        tr.op('dve', lambda: nc.vector.tensor_copy(out=sc[:, 0:16], in_=bank(bk)[:, 0:16]), reads=[PB(bk)], writes=[sk])
        tr.op('act', lambda: nc.scalar.activation(out=sc[:, 16:32], in_=sc[:, 0:16], func=AF.Exp), reads=[sk], writes=[sk])
        tr.op('dve', lambda: nc.vector.tensor_tensor(out=sc[:, 32:40], in0=sc[:, 8:16], in1=sc[:, 0:8], op=ALU.subtract), reads=[sk], writes=[sk])
        tr.op('act', lambda: nc.scalar.activation(out=sc[:, 40:48], in_=sc[:, 32:40], func=AF.Exp), reads=[sk], writes=[sk])
        tr.op('dve', lambda: nc.vector.tensor_tensor(out=sc[:, 48:56], in0=bsl, in1=sc[:, 16:24], op=ALU.mult), reads=[sk, 'gb'], writes=[sk])

    def chain_step(ch, d_, hg, tg, par, is_x):
        h0 = hg * 4
        b0, b1 = 2 * ch, 2 * ch + 1
        R = lambda n: (n, ch)
        Bf = lambda n: cb[(n, ch)]
        sc = scal[(d_, par)]
        sk = ('scal', d_, par)
        QT, KT, KTOK, VTOK = (ldb[(d_, par, nm)] for nm in ('qt', 'kt', 'ktok', 'vtok'))
        LQ, LK, LKT, LV = (('ld', d_, par, nm) for nm in ('qt', 'kt', 'ktok', 'vtok'))
        bcol = lambda h: gb[:, tg, 16 + d_ * 8 + h:16 + d_ * 8 + h + 1]
        mc = cs['mcum'][:, d_ * 128:(d_ + 1) * 128]
        bigm = cs['bigm'][:, d_ * 128:(d_ + 1) * 128]
        kA = 0 if d_ == 0 else 1
        sidx = lambda i: d_ * 8 + h0 + i
        bk = lambda b, i: bank(b)[:, i * 128:(i + 1) * 128]
        IDB = ('c', 'ident_b')
        for i in range(4):
            gcol = gb[:, tg, d_ * 8 + h0 + i:d_ * 8 + h0 + i + 1]
            tr.op('pe', lambda i=i, gcol=gcol: nc.tensor.matmul(bk(b0, i), lhsT=gcol.to_broadcast([128, 128]), rhs=mc, start=True, stop=False),
                  reads=['gb', ('c', 'mcum')], writes=[PB(b0)])
            tr.op('pe', lambda i=i: nc.tensor.matmul(bk(b0, i), lhsT=cs['ident_b'], rhs=bigm, start=False, stop=True),
                  reads=[('c', 'ident_b'), ('c', 'bigm')], writes=[PB(b0)])
        for i in range(4):
            tr.op('act', lambda i=i: nc.scalar.activation(out=Bf('Ex')[:, i, :], in_=bk(b0, i), func=AF.Exp, bias=sc[:, h0 + i:h0 + i + 1], scale=-1.0),
                  reads=[PB(b0), sk], writes=[R('Ex')])
        load['act'] += 1.3
        yield
        for i in range(4):
            tr.op('pe', lambda i=i: nc.tensor.matmul(bk(b1, i), lhsT=KT[:, h0 + i, :], rhs=KT[:, h0 + i, :], start=True, stop=True),
                  reads=[LK], writes=[PB(b1)])
        for i in range(4):
            tr.op('pe', lambda i=i: nc.tensor.matmul(bk(b0, i), lhsT=QT[:, h0 + i, :], rhs=KT[:, h0 + i, :], start=True, stop=True),
                  reads=[LQ, LK], writes=[PB(b0)])
        for i in range(4):
            tr.op('dve', lambda i=i: nc.vector.scalar_tensor_tensor(out=Bf('Af')[:, i, :], in0=bk(b1, i), scalar=bcol(h0 + i), in1=Bf('Ex')[:, i, :],
                                                                   op0=ALU.mult, op1=ALU.mult),
                  reads=[PB(b1), 'gb', R('Ex')], writes=[R('Af')])
        tr.op('dve', lambda: nc.vector.tensor_tensor(out=Bf('qk'), in0=f4(bank(b0)), in1=Bf('Ex'), op=ALU.mult), reads=[PB(b0), R('Ex')], writes=[R('qk')])
        LV_ = ('c', 'lvl')
        for l in range(1):
            tr.op('dve', lambda l=l: nc.vector.tensor_tensor(out=Bf('A%d' % l), in0=Bf('Af'), in1=lvlm(kA, l), op=ALU.mult),
                  reads=[R('Af'), LV_], writes=[R('A%d' % l)])
        load['dve'] += 1.0 + 0.65 + 0.6
        yield
        for i in range(4):
            tr.op('pe', lambda i=i: nc.tensor.transpose(bankbf(b1)[:, i * 128:(i + 1) * 128], Bf('A0')[:, i, :], cs['ident_b']),
                  reads=[R('A0'), IDB], writes=[PB(b1)])
        for i in range(4):
            tr.op('pe', lambda i=i: nc.tensor.transpose(bankbf(b0)[:, i * 128:(i + 1) * 128], Bf('qk')[:, i, :], cs['ident_b']),
                  reads=[R('qk'), IDB], writes=[PB(b0)])
        evac(Bf('PB'), f4(bankbf(b1)[:, 0:512]), [PB(b1)], [R('PB')])
        evac(Bf('qkT'), f4(bankbf(b0)[:, 0:512]), [PB(b0)], [R('qkT')])
        tr.op('dve', lambda: nc.vector.tensor_tensor(out=Bf('U'), in0=I4, in1=Bf('PB'), op=ALU.subtract), reads=[R('PB'), ('c', 'ident4_b')], writes=[R('U')])
        load['dve'] += 0.4
        yield
        pa_cur = 'A0'
        NLEV = 3
        for lev in range(NLEV):
            for i in range(4):
                tr.op('pe', lambda i=i, pa_cur=pa_cur: nc.tensor.matmul(bk(b0, i), lhsT=Bf('PB')[:, i, :], rhs=Bf(pa_cur)[:, i, :], start=True, stop=True),
                      reads=[R(pa_cur), R('PB')], writes=[PB(b0)])
            if lev < NLEV - 1:
                for i in range(4):
                    tr.op('pe', lambda i=i, pa_cur=pa_cur: nc.tensor.matmul(bk(b1, i), lhsT=Bf(pa_cur)[:, i, :], rhs=Bf('PB')[:, i, :], start=True, stop=True),
                          reads=[R(pa_cur), R('PB')], writes=[PB(b1)])
            evac(Bf('PA'), f4(bank(b0)), [PB(b0)], [R('PA')])
            if lev < NLEV - 1:
                evac(Bf('PB'), f4(bank(b1)), [PB(b1)], [R('PB')])
            pa_cur = 'PA'
            yield
            for i in range(4):
                tr.op('pe', lambda i=i: nc.tensor.matmul(bk(b0, i), lhsT=cs['ident_b'], rhs=Bf('U')[:, i, :], start=True, stop=False),
                      reads=[R('U'), IDB], writes=[PB(b0)])
                tr.op('pe', lambda i=i: nc.tensor.matmul(bk(b0, i), lhsT=Bf('PA')[:, i, :], rhs=Bf('U')[:, i, :], start=False, stop=True),
                      reads=[R('U'), R('PA')], writes=[PB(b0)])
            evac(Bf('U'), f4(bank(b0)), [PB(b0)], [R('U')])
            yield
        for l in (1, 2, 3):
            for i in range(4):
                tr.op('pe', lambda i=i: nc.tensor.transpose(bankbf(b1)[:, i * 128:(i + 1) * 128], Bf('U')[:, i, :], cs['ident_b']),
                      reads=[R('U'), IDB], writes=[PB(b1)])
            for i in range(4):
                tr.op('pe', lambda i=i, l=l: nc.tensor.matmul(bk(b0, i), lhsT=Bf('Af')[:, i, :], rhs=Bf('U')[:, i, :], start=True, stop=True),
                      reads=[R('Af'), R('U')], writes=[PB(b0)])
            evac(Bf('T'), f4(bankbf(b1)[:, 0:512]), [PB(b1)], [R('T')])
            tr.op('dve', lambda l=l: nc.vector.scalar_tensor_tensor(out=Bf('Xn'), in0=f4(bank(b0)), scalar=-1.0, in1=lvlm(1 - kA, l),
                                                                   op0=ALU.mult, op1=ALU.mult),
                  reads=[PB(b0), LV_], writes=[R('Xn')])
            load['dve'] += 0.7
            yield
            for i in range(4):
                tr.op('pe', lambda i=i: nc.tensor.matmul(bk(b0, i), lhsT=cs['ident_b'], rhs=Bf('U')[:, i, :], start=True, stop=False),
                      reads=[R('U'), IDB], writes=[PB(b0)])
                tr.op('pe', lambda i=i: nc.tensor.matmul(bk(b0, i), lhsT=Bf('T')[:, i, :], rhs=Bf('Xn')[:, i, :], start=False, stop=True),
                      reads=[R('T'), R('Xn')], writes=[PB(b0)])
            evac(Bf('U'), f4(bank(b0)), [PB(b0)], [R('U')])
            yield
        tr.op('dve', lambda: nc.vector.tensor_tensor(out=Bf('vb'), in0=VTOK[:, h0:h0 + 4, :], in1=bc4(gb[:, tg, 16 + d_ * 8 + h0:16 + d_ * 8 + h0 + 4]), op=ALU.mult),
              reads=[LV, 'gb'], writes=[R('vb')])
        tr.op('dve', lambda: nc.vector.tensor_tensor(out=Bf('kbg'), in0=KTOK[:, h0:h0 + 4, :], in1=bc4(sc[:, 48 + h0:48 + h0 + 4]), op=ALU.mult),
              reads=[LKT, sk], writes=[R('kbg')])
        tr.op('pool', lambda: nc.gpsimd.tensor_tensor(out=Bf('ktl'), in0=KTOK[:, h0:h0 + 4, :], in1=bc4(sc[:, 40 + h0:40 + h0 + 4]), op=ALU.mult),
              reads=[LKT, sk], writes=[R('ktl')])
        load['dve'] += 0.9
        for i in range(4):
            tr.op('pe', lambda i=i: nc.tensor.matmul(bk(b0, i), lhsT=Bf('U')[:, i, :], rhs=Bf('vb')[:, i, :], start=True, stop=True),
                  reads=[R('U'), R('vb')], writes=[PB(b0)])
        for i in range(4):
            tr.op('pe', lambda i=i: nc.tensor.matmul(bk(b1, i), lhsT=Bf('kbg')[:, i, :], rhs=Bf('U')[:, i, :], start=True, stop=True),
                  reads=[R('U'), R('kbg')], writes=[PB(b1)])
        evac(Bf('u'), f4(bank(b0)), [PB(b0)], [R('u')])
        evac(Bf('wT'), f4(bank(b1)), [PB(b1)], [R('wT')])
        yield
        SK = ('Sb', ch)
        for i in range(4):
            tr.op('pe', lambda i=i: nc.tensor.matmul(bk(b0, i), lhsT=Bf('wT')[:, i, :], rhs=Sb[:, sidx(i), :], start=True, stop=True),
                  reads=[R('wT'), SK], writes=[PB(b0)])
        if is_x:
            for i in range(4):
                tr.op('pe', lambda i=i: nc.tensor.matmul(bk(b1, i), lhsT=QT[:, h0 + i, :], rhs=Sb[:, sidx(i), :], start=True, stop=True),
                      reads=[LQ, SK], writes=[PB(b1)])
        tr.op('dve', lambda: nc.vector.tensor_tensor(out=Bf('vnew'), in0=Bf('u'), in1=f4(bank(b0)), op=ALU.subtract), reads=[PB(b0), R('u')], writes=[R('vnew')])
        load['dve'] += 0.65
        if is_x:
            for i in range(4):
                tr.op('act', lambda i=i: nc.scalar.activation(out=Bf('o1')[:, i, :], in_=bk(b1, i), func=AF.Copy, scale=sc[:, 16 + h0 + i:16 + h0 + i + 1]),
                      reads=[PB(b1), sk], writes=[R('o1')])
            load['act'] += 1.2
        yield
        if is_x:
            for i in range(4):
                tr.op('pe', lambda i=i: nc.tensor.matmul(bk(b1, i), lhsT=Bf('qkT')[:, i, :], rhs=Bf('vnew')[:, i, :], start=True, stop=True),
                      reads=[R('qkT'), R('vnew')], writes=[PB(b1)])
        for i in range(4):
            tr.op('pe', lambda i=i: nc.tensor.matmul(bk(b0, i), lhsT=Bf('ktl')[:, i, :], rhs=Bf('vnew')[:, i, :], start=True, stop=True),
                  reads=[R('ktl'), R('vnew')], writes=[PB(b0)])
        if is_x:
            tr.op('dve', lambda: nc.vector.tensor_tensor(out=f4(ost[(d_, par)][:, hg * 512:(hg + 1) * 512]), in0=f4(bank(b1)), in1=Bf('o1'), op=ALU.add),
                  reads=[PB(b1), R('o1')], writes=[('ost', d_, 0, hg)])
            load['dve'] += 0.65
        for i in range(4):
            tr.op('dve', lambda i=i: nc.vector.scalar_tensor_tensor(out=Sf[:, sidx(i), :], in0=Sf[:, sidx(i), :], scalar=sc[:, 24 + h0 + i:24 + h0 + i + 1],
                                                                   in1=bk(b0, i), op0=ALU.mult, op1=ALU.add),
                  reads=[PB(b0), sk, ('Sf', ch)], writes=[('Sf', ch)])
        load['dve'] += 1.2
        tr.op('act', lambda: nc.scalar.copy(out=Sb[:, d_ * 8 + h0:d_ * 8 + h0 + 4, :], in_=Sf[:, d_ * 8 + h0:d_ * 8 + h0 + 4, :]), reads=[('Sf', ch)], writes=[SK])
        load['act'] += 0.5
        yield

    nsteps = int(os.environ.get("MK_STEPS", "34"))
    issue_loads(0)
    for step in range(nsteps):
        par = step % 2
        if step + 1 < nsteps:
            issue_loads(step + 1)
        gens = []
        for d_ in range(2):
            tg = order[d_][step]
            prep(d_, tg, par)
        for d_ in range(2):
            tg = order[d_][step]
            for hg in range(2):
                gens.append(chain_step(d_ * 2 + hg, d_, hg, tg, par, tg >= 2))
        while gens:
            for g_ in list(gens):
                try:
                    next(g_)
                except StopIteration:
                    gens.remove(g_)
        for d_ in range(2):
            tg = order[d_][step]
            if tg >= 2:
                tr.dma('sp', o_d[d_, tg - 2], ost[(d_, par)], reads=[('ost', d_, 0, 0), ('ost', d_, 0, 1)], writes=['osc'])
    tr.barrier()

    if os.environ.get("MK_STOP", "") == "B2":
        tr.op('dve', lambda: nc.vector.memset(ost[(0, 0)], 0.0), writes=[('ost', 0, 0, 0)])
        for t in range(32):
            tr.dma('sp', out_d[t * 128:(t + 1) * 128, :], ost[(0, 0)], reads=[('ost', 0, 0, 0)], writes=['out'])
        tr.barrier()
        _CACHE['nc'] = nc
        print("instr waits", tr.nwait, "cnt", tr.cnt, "dma", tr.dn)
        return nc
    end_scope()
    new_scope()
    utk = sb("utk", [128, 32, D], BF16)
    for t in range(32):
        tr.dma('sp', utk[:, t, :], utok_d[t], reads=['utok'], writes=[('utk', t)])
    pmb = sb("pmb", [128, NM, 128], BF16)
    tr.dma('sp', pmb, pm_d, writes=['pmb'])
    woutb = sb("woutb", [128, 16, D], BF16)
    for m in range(16):
        tr.dma_w('pool', woutb[:, m, :], wout_d[:, m, :], writes=[('woutb', m)])
    pwb = sb("pwb", [128, 4, 2, 256], BF16)
    for g in range(4):
        tr.dma_w('pool', pwb[:, g, :, :], poolw_d[:, g, :, :], writes=['pwb'])
    for eb in range(8):
        tr.op('dve', lambda eb=eb: nc.vector.tensor_scalar(out=woutb[:, 8 + eb, :], in0=woutb[:, 8 + eb, :], scalar1=pscale[:, eb:eb + 1],
                                                           scalar2=None, op0=ALU.mult),
              reads=[('woutb', 8 + eb), 'pscale'], writes=[('woutb', 8 + eb)])
    gateB = sb("gateB", [128, D])
    tr.dma('sp', gateB, gate_d.partition_broadcast(128), reads=['gate_d'], writes=['gateB'])
    fgB = sb("fgB", [128, D])
    tr.dma('sp', fgB, fg_d.partition_broadcast(128), writes=['fgB'])
    ofb = [sb("ofb%d" % i, [128, D]) for i in range(2)]
    obb = [sb("obb%d" % i, [128, D]) for i in range(2)]
    xcb = [sb("xcb%d" % i, [128, D]) for i in range(3)]
    st3 = sb("st3", [128, 2, 8])
    zdb = [sb("zdb%d" % i, [128, 8, 128], BF16) for i in range(2)]
    zpb = [sb("zpb%d" % i, [128, 8, 128], BF16) for i in range(3)]
    onb2 = [sb("onb%d" % i, [128, 8, 128], BF16) for i in range(2)]
    yTb = [sb("yTb%d" % i, [128, 16, 128], BF16) for i in range(3)]
    dTb2 = [sb("dTb%d" % i, [128, 8, 128], BF16) for i in range(2)]
    resb = sb("resb", [128, D])
    outb = [sb("outb%d" % i, [128, D]) for i in range(1)]
    junk2 = sb("junk2", [128, 128], BF16)
    st2 = sb("st2", [128, 2, 32])
    def c_front(t):
        nonlocal nbk
        p = t % 2
        tr.dma('sp', ofb[p], o_d[0, t], reads=['osc'], writes=[('ofb', p)])
        tr.dma('sp', obb[p], o_d[1, t], reads=['osc'], writes=[('obb', p)])
        tr.dma('sp', zdb[p], zdn_d[t], reads=['zdn'], writes=[('zdb', p)])
        tr.dma('sp', zpb[t % 3], zpl_d[t], reads=['zpl'], writes=[('zpb', t % 3)])
        sk = ('st2', p)
        tr.op('dve', lambda p=p: nc.vector.tensor_tensor(out=ofb[p], in0=ofb[p], in1=obb[p], op=ALU.add), reads=[('ofb', p), ('obb', p)], writes=[('ofb', p)])
        for h in range(8):
            tr.op('act', lambda p=p, h=h: nc.scalar.activation(out=junk2, in_=ofb[p][:, h * 128:(h + 1) * 128], func=AF.Square,
                                                               accum_out=st2[:, p, h:h + 1]),
                  reads=[('ofb', p)], writes=['junk2', sk])
        tr.op('dve', lambda p=p: nc.vector.tensor_scalar(out=st2[:, p, 8:16], in0=st2[:, p, 0:8], scalar1=1.0 / 128, scalar2=EPS, op0=ALU.mult, op1=ALU.add),
              reads=[sk], writes=[sk])
        tr.op('act', lambda p=p: nc.scalar.activation(out=st2[:, p, 16:24], in_=st2[:, p, 8:16], func=AF.Sqrt), reads=[sk], writes=[sk])
        tr.op('dve', lambda p=p: nc.vector.reciprocal(out=st2[:, p, 24:32], in_=st2[:, p, 16:24]), reads=[sk], writes=[sk])
        tr.op('dve', lambda p=p: nc.vector.tensor_tensor(out=onb2[p], in0=ofb[p].rearrange("p (h k) -> p h k", h=8),
                                                        in1=st2[:, p, 24:32].unsqueeze(2).to_broadcast([128, 8, 128]), op=ALU.mult),
              reads=[('ofb', p), sk], writes=[('onb', p)])
    def c_mid(t):
        nonlocal nbk
        p = t % 2
        sk = ('st2', p)
        b = t % 2
        for h in range(8):
            tr.op('pe', lambda b=b, h=h: nc.tensor.transpose(bankbf(b)[:, h * 128:(h + 1) * 128], onb2[p][:, h, :], cs['ident_b']),
                  reads=[('onb', p), ('c', 'ident_b')], writes=[PB(b)])
        yT = yTb[t % 3]
        yk = ('yT', t % 3)
        tr.op('dve', lambda b=b, p=p, yT=yT: nc.vector.scalar_tensor_tensor(out=yT[:, 0:8, :], in0=bankbf(b).rearrange("p (h k) -> p h k", h=8), scalar=dng[:, 0:1],
                                                                           in1=zdb[p], op0=ALU.mult, op1=ALU.mult),
              reads=[PB(b), 'dng', ('zdb', p)], writes=[yk])
        for half in range(2):
            b = 2 + half
            for q in range(4):
                cbk = half * 4 + q
                g = cbk // 2
                lst = psched[t][g]
                for n_, (tin, mid) in enumerate(lst):
                    tr.op('pe', lambda b=b, q=q, cbk=cbk, tin=tin, mid=mid, n_=n_, lst=lst: nc.tensor.matmul(
                        bank(b)[:, q * 128:(q + 1) * 128], lhsT=utk[:, tin, cbk * 128:(cbk + 1) * 128], rhs=pmb[:, mid, :],
                        start=(n_ == 0), stop=(n_ == len(lst) - 1)),
                        reads=[('utk', tin), 'pmb'], writes=[PB(b)])
            e_ = 'act' if half == 0 else 'dve'
            if half == 0:
                tr.op('act', lambda b=b: nc.scalar.copy(out=dTb2[p][:, 0:4, :], in_=f4(bank(b))), reads=[PB(b)], writes=[('dTb', p, 0)])
            else:
                tr.op('dve', lambda b=b: nc.vector.tensor_copy(out=dTb2[p][:, 4:8, :], in_=f4(bank(b))), reads=[PB(b)], writes=[('dTb', p, 1)])
    def c_mid2(t):
        nonlocal nbk
        p = t % 2
        tr.dma('sp', xcb[t % 3], x_d[t * 128:(t + 1) * 128, :], writes=[('xcb', t % 3)])
        yT = yTb[t % 3]
        yk = ('yT', t % 3)
        for half in range(2):
            b = 4 + half
            for q in range(4):
                eb = half * 4 + q
                g = eb // 2
                for cc in range(2):
                    tr.op('pe', lambda b=b, q=q, eb=eb, g=g, cc=cc: nc.tensor.matmul(
                        bank(b)[:, q * 128:(q + 1) * 128], lhsT=pwb[:, g, cc, (eb % 2) * 128:(eb % 2 + 1) * 128], rhs=dTb2[p][:, g * 2 + cc, :],
                        start=(cc == 0), stop=(cc == 1)),
                        reads=['pwb', ('dTb', p, 0), ('dTb', p, 1)], writes=[PB(b)])
            tr.op('dve', lambda b=b, half=half, yT=yT: nc.vector.tensor_tensor(
                out=yT[:, 8 + half * 4:8 + half * 4 + 4, :], in0=f4(bank(b)), in1=zpb[t % 3][:, half * 4:half * 4 + 4, :], op=ALU.mult),
                reads=[PB(b), ('zpb', t % 3)], writes=[yk])
    def c_back(t):
        nonlocal nbk
        p = t % 2
        sk = ('st3', p)
        yT = yTb[t % 3]
        yk = ('yT', t % 3)
        for half in range(2):
            b = 6 + half
            for m in range(16):
                tr.op('pe', lambda b=b, m=m, half=half, yT=yT: nc.tensor.matmul(bank(b), lhsT=yT[:, m, :], rhs=woutb[:, m, half * 512:(half + 1) * 512],
                                                                               start=(m == 0), stop=(m == 15)),
                      reads=[yk, ('woutb', m)], writes=[PB(b)])
            tr.op('dve', lambda b=b, half=half: nc.vector.tensor_tensor(out=resb[:, half * 512:(half + 1) * 512], in0=bank(b), in1=gateB[:, half * 512:(half + 1) * 512],
                                                                        op=ALU.mult),
                  reads=[PB(b), 'gateB'], writes=[('resb', half)])
            tr.op('dve', lambda half=half, p=p: nc.vector.tensor_tensor(out=resb[:, half * 512:(half + 1) * 512], in0=resb[:, half * 512:(half + 1) * 512],
                                                                         in1=xcb[t % 3][:, half * 512:(half + 1) * 512], op=ALU.add),
                  reads=[('resb', half), ('xcb', t % 3)], writes=[('resb', half)])
        sk2 = ('st2b', p)
        tr.op('act', lambda p=p: nc.scalar.activation(out=outb[0], in_=resb, func=AF.Square, accum_out=st3[:, p, 0:1]),
              reads=[('resb', 0), ('resb', 1)], writes=[('outb', 0), sk])
        tr.op('dve', lambda p=p: nc.vector.tensor_scalar(out=st3[:, p, 1:2], in0=st3[:, p, 0:1], scalar1=1.0 / D, scalar2=EPS, op0=ALU.mult, op1=ALU.add),
              reads=[sk], writes=[sk])
        tr.op('act', lambda p=p: nc.scalar.activation(out=st3[:, p, 2:3], in_=st3[:, p, 1:2], func=AF.Sqrt), reads=[sk], writes=[sk])
        tr.op('dve', lambda p=p: nc.vector.reciprocal(out=st3[:, p, 3:4], in_=st3[:, p, 2:3]), reads=[sk], writes=[sk])
        tr.op('dve', lambda p=p: nc.vector.scalar_tensor_tensor(out=outb[0], in0=resb, scalar=st3[:, p, 3:4], in1=fgB, op0=ALU.mult, op1=ALU.mult),
              reads=[('resb', 0), ('resb', 1), sk, 'fgB'], writes=[('outb', 0)])
        tr.dma('act', out_d[t * 128:(t + 1) * 128, :], outb[0], reads=[('outb', 0)], writes=['out'])
    for t in range(32 + 3):
        if t < 32:
            c_front(t)
        if 0 <= t - 1 < 32:
            c_mid(t - 1)
        if 0 <= t - 2 < 32:
            c_mid2(t - 2)
        if t - 3 >= 0:
            c_back(t - 3)
    tr.barrier()
    end_scope()
    _CACHE['nc'] = nc
    print("instr waits", tr.nwait, "cnt", tr.cnt, "dma", tr.dn)
    return nc


def kernel(**inp):
    nc = build()
    C = _consts()
    pmats, _ = _pool_mats()
    f = lambda a: np.ascontiguousarray(np.asarray(a, dtype=np.float32))
    w_mod = f(inp["w_mod"][0]).reshape(8, 128, 3072).transpose(1, 0, 2)
    w_in = f(inp["w_in"][0]).reshape(8, 128, 6176).transpose(1, 0, 2)
    conv_w = f(inp["conv_w"][0]).reshape(5, 24, 128).transpose(2, 1, 0)
    pool_w = f(inp["pool_w"][0]).reshape(4, 2, 128, 256).transpose(2, 0, 1, 3)
    w_out = f(inp["w_out"][0]).reshape(16, 128, 1024).transpose(1, 0, 2)
    shared = {
        "w_mod": np.ascontiguousarray(w_mod), "b_mod": f(inp["b_mod"]).reshape(1, 3072),
        "norm_gain": f(inp["norm_gain"]).reshape(1, D), "w_in": np.ascontiguousarray(w_in),
        "conv_w": np.ascontiguousarray(conv_w), "a_log": f(inp["a_log"]).reshape(1, 16),
        "dt_bias": f(inp["dt_bias"]).reshape(1, 16), "dn_gain": f(inp["dn_norm_gain"]).reshape(128, 1),
        "pool_w": np.ascontiguousarray(pool_w), "pool_scale": np.ascontiguousarray(f(inp["pool_scale"]).reshape(8, 128).T),
        "w_out": np.ascontiguousarray(w_out), "final_gain": f(inp["final_gain"]).reshape(1, D),
        "pmats": np.ascontiguousarray(pmats),
    }
    for k, v in C.items():
        shared["c_" + k] = np.ascontiguousarray(v)
    x = f(inp["x"])
    ctx = f(inp["ctx"])
    c = f(inp["c"])
    cc = f(inp["c_ctx"])
    in_maps = []
    for b in range(8):
        m = dict(shared)
        m["x"] = x[b]
        m["ctx"] = ctx[b]
        m["ccol"] = np.ascontiguousarray(np.concatenate([c[b].reshape(8, 128).T, cc.reshape(8, 128).T], 1))
        in_maps.append(m)
    res = run_bass_kernel_spmd(nc, in_maps, core_ids=list(range(8)))
    _CACHE['res'] = res
    return np.stack([np.asarray(r["out"], dtype=np.float32) for r in res.results], 0)
```
